# Optimizing a Trainium2 kernel written in Bass

```python
import jax, jax.numpy as jnp
from jax import lax
import numpy as np

D_MODEL = 1024
BATCH = 8
SEQ = 8192
DEPTH = 2

D_MIX = D_MODEL
RET_HEADS = 4
RET_QK_DIM = 64
RET_V_DIM = 128
RET_CHUNK = 256
RET_WIDTH = RET_HEADS * RET_V_DIM
POOL_GROUPS = 4
POOL_WINDOWS = (2, 4, 8, 16)
POOL_GROUP_DIM = 64
POOL_WIDTH = POOL_GROUPS * POOL_GROUP_DIM
MOBA_HEADS = 4
MOBA_HEAD_DIM = 64
MOBA_WIDTH = MOBA_HEADS * MOBA_HEAD_DIM
MOBA_BLOCK = 256
MOBA_TOPK = 3
MOBA_Q_BLOCK = 32
IN_COLS = 2 * RET_HEADS * RET_QK_DIM + 2 * RET_WIDTH + POOL_WIDTH + 3 * MOBA_WIDTH
D_FF = 2816
PLE_DIM = 256
EPS = 1e-6

kernel_name = "hybrid_retention_pool_moba_macaron_block"


def rmsnorm(x, g):
    xf = x.astype(jnp.float32)
    y = xf * lax.rsqrt(jnp.mean(xf * xf, axis=-1, keepdims=True) + EPS)
    return (y * g.astype(jnp.float32)).astype(x.dtype)


def swiglu(h, w_gate, w_up, w_down):
    return (jax.nn.silu(h @ w_gate) * (h @ w_up)) @ w_down


def pad_to_multiple(t, mult, axis):
    n = t.shape[axis]
    pad = (-n) % mult
    if pad == 0:
        return t
    widths = [(0, 0)] * t.ndim
    widths[axis] = (0, pad)
    return jnp.pad(t, widths)


def rotate_every_two(t):
    t1 = t[..., ::2]
    t2 = t[..., 1::2]
    return jnp.stack((-t2, t1), axis=-1).reshape(t.shape)


def retention(q, k, v, g):
    B, S, H, dk = q.shape
    dv = v.shape[-1]
    C = RET_CHUNK
    q = pad_to_multiple(q.astype(jnp.float32), C, 1)
    k = pad_to_multiple(k.astype(jnp.float32), C, 1)
    v = pad_to_multiple(v.astype(jnp.float32), C, 1)
    Sp = q.shape[1]
    NC = Sp // C
    pos = jnp.arange(Sp, dtype=jnp.float32)
    angle = 1.0 / (10000.0 ** jnp.linspace(0.0, 1.0, dk // 2, dtype=jnp.float32))
    angle = jnp.repeat(angle, 2)
    ang = pos[:, None] * angle[None, :]
    sin = jnp.sin(ang)[:, None, :]
    cos = jnp.cos(ang)[:, None, :]
    q = q * cos + rotate_every_two(q) * sin
    k = (k * cos + rotate_every_two(k) * sin) * (dk ** -0.5)
    log_gamma = jnp.log(1.0 - jnp.power(2.0, -5.0 - jnp.arange(H, dtype=jnp.float32)))
    idx = jnp.arange(C, dtype=jnp.float32)
    diff = idx[:, None] - idx[None, :]
    dmask = jnp.where(diff >= 0, jnp.exp(log_gamma[:, None, None] * jnp.maximum(diff, 0.0)), 0.0)
    xi = jnp.exp(log_gamma[:, None] * (idx + 1.0))[None, :, :, None]
    zeta = jnp.exp(log_gamma[:, None] * (C - 1.0 - idx))[None, :, :, None]
    gC = jnp.exp(log_gamma * C)[None, :, None, None]

    def to_chunks(t):
        return t.reshape(B, NC, C, H, t.shape[-1]).transpose(1, 0, 3, 2, 4)

    def step(R, inp):
        qj, kj, vj = inp
        inner = jnp.einsum('bhid,bhmd->bhim', qj, kj) * dmask
        out = (jnp.einsum('bhim,bhme->bhie', inner, vj)
               + jnp.einsum('bhid,bhde->bhie', qj, R) * xi)
        R = gC * R + jnp.einsum('bhmd,bhme->bhde', kj * zeta, vj)
        return R, out

    R0 = jnp.zeros((B, H, dk, dv), jnp.float32)
    _, o = lax.scan(step, R0, (to_chunks(q), to_chunks(k), to_chunks(v)))
    o = o.transpose(1, 0, 3, 2, 4).reshape(B, Sp, H, dv)[:, :S]
    mu = jnp.mean(o, axis=-1, keepdims=True)
    var = jnp.mean(jnp.square(o - mu), axis=-1, keepdims=True)
    o = (o - mu) * lax.rsqrt(var + EPS)
    o = o * jax.nn.silu(g.astype(jnp.float32))
    return o.reshape(B, S, H * dv)


def multiscale_pool(u, w, scale):
    B, S, _ = u.shape
    uf = u.astype(jnp.float32).reshape(B, S, POOL_GROUPS, POOL_GROUP_DIM)
    cs = jnp.concatenate([jnp.zeros((B, 1, POOL_GROUPS, POOL_GROUP_DIM), jnp.float32),
                          jnp.cumsum(uf, axis=1)], axis=1)
    t = jnp.arange(S)
    outs = []
    for gi, win in enumerate(POOL_WINDOWS):
        start = jnp.maximum(t + 1 - win, 0)
        cnt = (t + 1 - start).astype(jnp.float32)
        csg = cs[:, :, gi]
        wsum = csg[:, 1:] - csg[:, start]
        outs.append(wsum / cnt[None, :, None] - uf[:, :, gi])
    pooled = jnp.stack(outs, axis=2)
    y = jnp.einsum('bsgc,gcd->bsgd', pooled, w.astype(jnp.float32)).reshape(B, S, POOL_WIDTH)
    return y * scale.astype(jnp.float32)


def moba_attention(q, k, v):
    B, S, H, dh = q.shape
    BLK = MOBA_BLOCK
    QB = MOBA_Q_BLOCK
    q = pad_to_multiple(q.astype(jnp.float32).transpose(0, 2, 1, 3), BLK, 2) * (dh ** -0.5)
    k = pad_to_multiple(k.astype(jnp.float32).transpose(0, 2, 1, 3), BLK, 2)
    v = pad_to_multiple(v.astype(jnp.float32).transpose(0, 2, 1, 3), BLK, 2)
    Sp = q.shape[2]
    NB = Sp // BLK
    kb = k.reshape(B, H, NB, BLK, dh)
    vb = v.reshape(B, H, NB, BLK, dh)
    kmean = jnp.mean(kb, axis=3)
    gate = jnp.einsum('bhsd,bhnd->bhsn', q, kmean)
    qblk = jnp.arange(Sp) // BLK
    past = jnp.arange(NB)[None, :] < qblk[:, None]
    gate = jnp.where(past, gate, -jnp.inf)
    k_sel = min(MOBA_TOPK, NB)
    _, sel = lax.top_k(gate, k_sel)
    nqb = Sp // QB
    qs = q.reshape(B, H, nqb, QB, dh).transpose(2, 0, 1, 3, 4)
    sels = sel.reshape(B, H, nqb, QB, k_sel).transpose(2, 0, 1, 3, 4)
    bi = jnp.arange(B)[:, None, None, None]
    hi = jnp.arange(H)[None, :, None, None]

    def one_block(args):
        j, qj, selj = args
        q0 = j * QB
        own = q0 // BLK
        qpos = q0 + jnp.arange(QB)
        kpos = own * BLK + jnp.arange(BLK)
        k_own = lax.dynamic_index_in_dim(kb, own, axis=2, keepdims=False)
        v_own = lax.dynamic_index_in_dim(vb, own, axis=2, keepdims=False)
        k_g = kb[bi, hi, selj]
        v_g = vb[bi, hi, selj]
        s_own = jnp.einsum('bhqd,bhkd->bhqk', qj, k_own)
        s_own = jnp.where(kpos[None, :] <= qpos[:, None], s_own, -jnp.inf)
        s_sel = jnp.einsum('bhqd,bhqnkd->bhqnk', qj, k_g)
        valid = jnp.arange(k_sel) < own
        s_sel = jnp.where(valid[:, None], s_sel, -jnp.inf).reshape(B, H, QB, k_sel * BLK)
        probs = jax.nn.softmax(jnp.concatenate([s_sel, s_own], axis=-1), axis=-1)
        p_sel = probs[..., :k_sel * BLK].reshape(B, H, QB, k_sel, BLK)
        p_own = probs[..., k_sel * BLK:]
        return (jnp.einsum('bhqnk,bhqnkd->bhqd', p_sel, v_g)
                + jnp.einsum('bhqk,bhkd->bhqd', p_own, v_own))

    o = lax.map(one_block, (jnp.arange(nqb), qs, sels))
    o = o.transpose(1, 0, 3, 2, 4).reshape(B, Sp, H * dh)[:, :S]
    return o


def setup_inputs(seed: int = 0) -> dict:
    key = jax.random.key(seed)
    ks = jax.random.split(key, 20)
    f32 = jnp.float32

    def nrm(k, shape, scale):
        return jax.random.normal(k, shape, f32) * scale

    return {
        "x": nrm(ks[0], (BATCH, SEQ, D_MODEL), 1.0),
        "p": nrm(ks[1], (DEPTH, BATCH, SEQ, PLE_DIM), 1.0),
        "norm_ffn1": 1.0 + nrm(ks[2], (DEPTH, D_MODEL), 0.02),
        "ffn1_w_gate": nrm(ks[3], (DEPTH, D_MODEL, D_FF), D_MODEL ** -0.5),
        "ffn1_w_up": nrm(ks[4], (DEPTH, D_MODEL, D_FF), D_MODEL ** -0.5),
        "ffn1_w_down": nrm(ks[5], (DEPTH, D_FF, D_MODEL), D_FF ** -0.5),
        "norm_mix": 1.0 + nrm(ks[6], (DEPTH, D_MODEL), 0.02),
        "w_in": nrm(ks[7], (DEPTH, D_MODEL, IN_COLS), D_MODEL ** -0.5),
        "pool_w": nrm(ks[8], (DEPTH, POOL_GROUPS, POOL_GROUP_DIM, POOL_GROUP_DIM), POOL_GROUP_DIM ** -0.5),
        "pool_scale": 1.0 + nrm(ks[9], (DEPTH, POOL_WIDTH), 0.1),
        "w_out": nrm(ks[10], (DEPTH, D_MIX, D_MODEL), D_MIX ** -0.5),
        "norm_ffn2": 1.0 + nrm(ks[11], (DEPTH, D_MODEL), 0.02),
        "ffn2_w_gate": nrm(ks[12], (DEPTH, D_MODEL, D_FF), D_MODEL ** -0.5),
        "ffn2_w_up": nrm(ks[13], (DEPTH, D_MODEL, D_FF), D_MODEL ** -0.5),
        "ffn2_w_down": nrm(ks[14], (DEPTH, D_FF, D_MODEL), D_FF ** -0.5),
        "norm_ple": 1.0 + nrm(ks[15], (DEPTH, D_MODEL), 0.02),
        "ple_w_gate": nrm(ks[16], (DEPTH, D_MODEL, D_MODEL), D_MODEL ** -0.5),
        "ple_w_proj": nrm(ks[17], (DEPTH, PLE_DIM, D_MODEL), PLE_DIM ** -0.5),
        "norm_final": 1.0 + nrm(ks[18], (D_MODEL,), 0.02),
    }


def reference(x, p, norm_ffn1, ffn1_w_gate, ffn1_w_up, ffn1_w_down, norm_mix, w_in,
              pool_w, pool_scale, w_out, norm_ffn2, ffn2_w_gate, ffn2_w_up, ffn2_w_down,
              norm_ple, ple_w_gate, ple_w_proj, norm_final):
    B, S, _ = x.shape
    dqk = RET_HEADS * RET_QK_DIM
    offs = np.cumsum([dqk, dqk, RET_WIDTH, RET_WIDTH, POOL_WIDTH, MOBA_WIDTH, MOBA_WIDTH]).tolist()
    for i in range(DEPTH):
        h = rmsnorm(x, norm_ffn1[i])
        x = x + 0.5 * swiglu(h, ffn1_w_gate[i], ffn1_w_up[i], ffn1_w_down[i])
        h = rmsnorm(x, norm_mix[i])
        z = h @ w_in[i]
        rq, rk, rv, rg, pu, mq, mk, mv = jnp.split(z, offs, axis=-1)
        y_ret = retention(rq.reshape(B, S, RET_HEADS, RET_QK_DIM),
                          rk.reshape(B, S, RET_HEADS, RET_QK_DIM),
                          rv.reshape(B, S, RET_HEADS, RET_V_DIM),
                          rg.reshape(B, S, RET_HEADS, RET_V_DIM))
        y_pool = multiscale_pool(pu, pool_w[i], pool_scale[i])
        y_moba = moba_attention(mq.reshape(B, S, MOBA_HEADS, MOBA_HEAD_DIM),
                                mk.reshape(B, S, MOBA_HEADS, MOBA_HEAD_DIM),
                                mv.reshape(B, S, MOBA_HEADS, MOBA_HEAD_DIM))
        y = jnp.concatenate([y_ret, y_pool, y_moba], axis=-1).astype(x.dtype)
        x = x + y @ w_out[i]
        h = rmsnorm(x, norm_ffn2[i])
        x = x + 0.5 * swiglu(h, ffn2_w_gate[i], ffn2_w_up[i], ffn2_w_down[i])
        h = rmsnorm(x, norm_ple[i])
        x = x + jax.nn.sigmoid(h @ ple_w_gate[i]) * (p[i] @ ple_w_proj[i])
    return rmsnorm(x, norm_final)
```

```python
import contextlib
import numpy as np
import concourse.bass as bass
import concourse.mybir as mybir
from concourse.bass_utils import run_bass_kernel_spmd

F32 = mybir.dt.float32
BF16 = mybir.dt.bfloat16
AF = mybir.ActivationFunctionType
ALU = mybir.AluOpType
AX = mybir.AxisListType

D = 1024
S = 8192
NCORES = 8
DEPTH = 2
DFF = 2816
NF = DFF // 128
KD = D // 128
PLE = 256
EPS = 1e-6
TT = 512
NSUB = TT // 128


class _Op:
    __slots__ = ("eng", "fn", "deps", "is_dma", "needs_inc", "ordinal", "idx", "dma_n")

    def __init__(self, eng, fn, is_dma):
        self.eng = eng
        self.fn = fn
        self.deps = []
        self.is_dma = is_dma
        self.needs_inc = False
        self.ordinal = None
        self.idx = None
        self.dma_n = None


ENGINES = ("pe", "act", "dve", "pool", "sp")
PSUM_KEYS = {"pz", "ps", "po", "tp", "tpp", "pg", "pu", "pd", "pj"}
DMA_K = 8


class Sched:
    def __init__(self, nc, sems, dma_sems, counts, dma_counts):
        self.nc = nc
        self.sems = sems
        self.dma_sems = dma_sems
        self.counts = counts
        self.dma_counts = dma_counts
        self.ops = []
        self.last_w = {}
        self.readers = {}

    def _add(self, eng, fn, reads, writes, is_dma):
        ps_r = [r for r in reads if (r[0] if isinstance(r, tuple) else r) in PSUM_KEYS]
        if ps_r:
            reads = [r for r in reads if r not in ps_r]
            writes = list(writes) + [r for r in ps_r if r not in writes]
        op = _Op(eng, fn, is_dma)
        op.idx = len(self.ops)
        deps = {}
        for r in reads:
            w = self.last_w.get(r)
            if w is not None:
                deps[w.idx] = w
        for wkey in writes:
            w = self.last_w.get(wkey)
            if w is not None:
                deps[w.idx] = w
            for rd in self.readers.get(wkey, {}).values():
                deps[rd.idx] = rd
        deps.pop(op.idx, None)
        for d in deps.values():
            if (not d.is_dma) and (not is_dma) and d.eng == "pe" and eng == "pe":
                continue
            op.deps.append(d)
            d.needs_inc = True
        for r in reads:
            self.readers.setdefault(r, {})[(eng, is_dma)] = op
        for wkey in writes:
            self.last_w[wkey] = op
            self.readers[wkey] = {}
        self.ops.append(op)
        return op

    def op(self, eng, fn, reads=(), writes=()):
        return self._add(eng, fn, reads, writes, False)

    def dma(self, eng, fn, reads=(), writes=()):
        return self._add(eng, fn, reads, writes, True)

    def _token(self, d):
        if d.is_dma:
            n = d.dma_n
            return self.dma_sems[d.eng][n % DMA_K], 16 * (n // DMA_K + 1)
        return self.sems[d.eng], d.ordinal

    def emit(self, block, waited):
        for op in self.ops:
            if op.is_dma:
                op.dma_n = self.dma_counts[op.eng]
                self.dma_counts[op.eng] += 1
            elif op.needs_inc:
                self.counts[op.eng] += 1
                op.ordinal = self.counts[op.eng]
        per_eng = {e: [o for o in self.ops if o.eng == e] for e in ENGINES}
        final_tokens = []
        for e in ENGINES:
            last = None
            for o in reversed(per_eng[e]):
                if not o.is_dma:
                    last = o
                    break
            if last is not None:
                if not last.needs_inc:
                    last.needs_inc = True
                    self.counts[e] += 1
                    last.ordinal = self.counts[e]
                final_tokens.append((self.sems[e], last.ordinal))
            dmas = [o for o in per_eng[e] if o.is_dma]
            for o in dmas[-DMA_K:]:
                final_tokens.append(self._token(o))

        def run(e, eng_obj):
            wd = waited[e]
            for op in per_eng[e]:
                for d in op.deps:
                    sem, val = self._token(d)
                    if wd.get(sem.num, 0) < val:
                        eng_obj.wait_ge(sem, val)
                        wd[sem.num] = val
                if op.is_dma and op.dma_n >= DMA_K:
                    sem = self.dma_sems[e][op.dma_n % DMA_K]
                    val = 16 * (op.dma_n // DMA_K)
                    if wd.get(sem.num, 0) < val:
                        eng_obj.wait_ge(sem, val)
                        wd[sem.num] = val
                inst = op.fn(eng_obj)
                if op.is_dma:
                    inst.then_inc(self.dma_sems[e][op.dma_n % DMA_K], 16)
                elif op.needs_inc:
                    inst.then_inc(self.sems[e], 1)
            for sem, val in final_tokens:
                if wd.get(sem.num, 0) < val:
                    eng_obj.wait_ge(sem, val)
                    wd[sem.num] = val

        @block.tensor
        def _(eng):
            run("pe", eng)

        @block.scalar
        def _(eng):
            run("act", eng)

        @block.vector
        def _(eng):
            run("dve", eng)

        @block.gpsimd
        def _(eng):
            run("pool", eng)

        @block.sync
        def _(eng):
            run("sp", eng)


class _Rec:
    def __init__(self):
        self.items = []

    def op(self, eng, fn, reads=(), writes=()):
        self.items.append((False, eng, fn, reads, writes))

    def dma(self, eng, fn, reads=(), writes=()):
        self.items.append((True, eng, fn, reads, writes))


def _merge_play(sch, a, b):
    na, nb = len(a), len(b)
    ia = ib = 0
    while ia < na or ib < nb:
        if ib >= nb or (ia < na and ia * nb <= ib * na):
            it = a[ia]
            ia += 1
        else:
            it = b[ib]
            ib += 1
        (sch.dma if it[0] else sch.op)(it[1], it[2], it[3], it[4])


class Ctx:
    def __init__(self, nc, stack):
        self.nc = nc
        self.stack = stack
        self.sems = {e: stack.enter_context(nc.semaphore("s_" + e)) for e in ENGINES}
        self.dma_sems = {
            e: [stack.enter_context(nc.semaphore("d_%s%d" % (e, i))) for i in range(DMA_K)]
            for e in ("sp", "pool", "act")
        }
        self.counts = {e: 0 for e in ENGINES}
        self.dma_counts = {e: 0 for e in ("sp", "pool", "act")}
        self.waited = {e: {} for e in ENGINES}

    def phase(self, body):
        nc = self.nc
        self.nphase = getattr(self, "nphase", 0) + 1
        pfx = "p%d_" % self.nphase
        with contextlib.ExitStack() as ps:
            def sb(name, shape, dt):
                return ps.enter_context(nc.sbuf_tensor(pfx + name, list(shape), dt))

            def pp(name, shape, dt=F32):
                return ps.enter_context(nc.psum_tensor(pfx + name, list(shape), dt))

            sch = Sched(nc, self.sems, self.dma_sems, self.counts, self.dma_counts)
            body(sch, sb, pp)
            with nc.Block() as block:
                sch.emit(block, self.waited)


def ffn_phase(ctx, tag, x_in, x_out, g_ap, wg_ap, wu_ap, wd_ap, ident_dram, ntiles):
    def body(sch, sb, pp):
        wg = sb("wg", [128, KD, DFF], BF16)
        wu = sb("wu", [128, KD, DFF], BF16)
        wd = sb("wd", [128, NF, D], BF16)
        gbc = sb("gbc", [128, D], F32)
        ident = sb("ident", [128, 128], BF16)
        epsb = sb("epsb", [128, 1], F32)
        xt = [sb("xt%d" % i, [128, NSUB, D], F32) for i in range(2)]
        sq = sb("sq", [128, D], BF16)
        ss = sb("ss", [128, NSUB], F32)
        rstd = sb("rstd", [128, NSUB], F32)
        hb = [sb("hb%d" % i, [128, D], BF16) for i in range(2)]
        hT = sb("hT", [128, KD, TT], BF16)
        sil = [sb("sil%d" % i, [128, TT], BF16) for i in range(2)]
        act = sb("act", [128, NF, TT], BF16)
        tp = [pp("tp%d" % i, [128, D], BF16) for i in range(2)]
        pg = [pp("pg%d" % i, [128, TT]) for i in range(2)]
        pu = [pp("pu%d" % i, [128, TT]) for i in range(2)]
        pd = [pp("pd%d" % i, [128, 512]) for i in range(2)]

        CB = [(0, 3), (3, 8), (8, 15), (15, 22)]
        for cb, (f0, f1) in enumerate(CB):
            for (wt, wap, nm) in ((wg, wg_ap, "wg"), (wu, wu_ap, "wu")):
                sch.dma("pool", lambda e, wt=wt, wap=wap, f0=f0, f1=f1: e.dma_start(
                    out=wt[:, :, f0 * 128:f1 * 128],
                    in_=wap[:, f0 * 128:f1 * 128].rearrange("(k p) f -> p k f", p=128)),
                    writes=[(nm, cb)])

        def cb_of(f):
            for cb, (f0, f1) in enumerate(CB):
                if f0 <= f < f1:
                    return cb
        for f in range(NF):
            sch.dma("pool", lambda e, f=f: e.dma_start(out=wd[:, f, :], in_=wd_ap[f * 128:(f + 1) * 128, :]),
                    writes=[("wd", f)])
        sch.dma("sp", lambda e: e.dma_start(out=gbc[:], in_=g_ap.partition_broadcast(128)), writes=["gbc"])
        sch.dma("sp", lambda e: e.dma_start(out=ident[:], in_=ident_dram), writes=["ident"])
        sch.op("dve", lambda e: e.memset(epsb[:], EPS), writes=["epsb"])

        def load(t):
            b = t % 2
            src = x_in[t * TT:(t + 1) * TT, :].rearrange("(s p) d -> p s d", p=128)
            sch.dma("sp", lambda e: e.dma_start(out=xt[b][:], in_=src), writes=[("xt", b)])

        def norm_stats(t):
            b = t % 2
            for s in range(NSUB):
                sch.op("act", lambda e, s=s: e.activation(out=sq[:], in_=xt[b][:, s, :], func=AF.Square,
                                                         accum_out=ss[:, s:s + 1]),
                       reads=[("xt", b)], writes=["sq", ("ss", s)])
            sch.op("act", lambda e: e.activation(out=rstd[:], in_=ss[:], func=AF.Sqrt, bias=epsb[:],
                                                 scale=1.0 / D),
                   reads=[("ss", s) for s in range(NSUB)] + ["epsb"], writes=["rstd0"])
            sch.op("dve", lambda e: e.reciprocal(out=rstd[:], in_=rstd[:]), reads=["rstd0"], writes=["rstd"])
            for s in range(2):
                norm_h(t, s)

        def norm_h(t, s):
            b = t % 2
            hbuf = hb[s % 2]
            sch.op("dve", lambda e: e.scalar_tensor_tensor(
                out=hbuf[:], in0=xt[b][:, s, :], scalar=rstd[:, s:s + 1], in1=gbc[:],
                op0=ALU.mult, op1=ALU.mult),
                reads=[("xt", b), "rstd", "gbc"], writes=[("hb", s % 2)])

        def norm_tr(t):
            for s in range(NSUB):
                if s >= 2:
                    norm_h(t, s)
                hbuf = hb[s % 2]
                tpb = tp[s % 2]

                def tr(e, hbuf=hbuf, tpb=tpb):
                    inst = None
                    for k in range(KD):
                        inst = e.transpose(out=tpb[:, k * 128:(k + 1) * 128],
                                           in_=hbuf[:, k * 128:(k + 1) * 128], identity=ident[:])
                    return inst
                sch.op("pe", tr, reads=[("hb", s % 2), "ident"], writes=[("tp", s % 2)])
                sch.op("act", lambda e, s=s, tpb=tpb: e.copy(
                    out=hT[:, :, s * 128:(s + 1) * 128], in_=tpb[:].rearrange("p (k t) -> p k t", k=KD)),
                    reads=[("tp", s % 2)], writes=[("hT", s)])

        def gate_up(t):
            for f in range(NF):
                b = f % 2

                def mm(e, f=f, b=b):
                    inst = None
                    for k in range(KD):
                        e.matmul(pg[b][:], lhsT=wg[:, k, f * 128:(f + 1) * 128], rhs=hT[:, k, :],
                                 start=(k == 0), stop=(k == KD - 1))
                    for k in range(KD):
                        inst = e.matmul(pu[b][:], lhsT=wu[:, k, f * 128:(f + 1) * 128], rhs=hT[:, k, :],
                                        start=(k == 0), stop=(k == KD - 1))
                    return inst
                sch.op("pe", mm,
                       reads=[("wg", cb_of(f)), ("wu", cb_of(f))] + [("hT", s) for s in range(NSUB)],
                       writes=[("pg", b), ("pu", b)])
                sch.op("act", lambda e, b=b: e.activation(out=sil[b][:], in_=pg[b][:], func=AF.Silu),
                       reads=[("pg", b)], writes=[("sil", b)])
                sch.op("dve", lambda e, f=f, b=b: e.tensor_tensor(out=act[:, f, :], in0=pu[b][:], in1=sil[b][:],
                                                                 op=ALU.mult),
                       reads=[("pu", b), ("sil", b)], writes=[("act", f)])

        def down(t):
            xb = t % 2
            for s in range(NSUB):
                for hlf in range(2):
                    b = (s * 2 + hlf) % 2

                    def mm(e, s=s, hlf=hlf, b=b):
                        inst = None
                        for f in range(NF):
                            inst = e.matmul(pd[b][:], lhsT=act[:, f, s * 128:(s + 1) * 128],
                                            rhs=wd[:, f, hlf * 512:(hlf + 1) * 512],
                                            start=(f == 0), stop=(f == NF - 1))
                        return inst
                    sch.op("pe", mm, reads=[("act", f) for f in range(NF)] + [("wd", f) for f in range(NF)],
                           writes=[("pd", b)])
                    sch.op("dve", lambda e, s=s, hlf=hlf, b=b: e.scalar_tensor_tensor(
                        out=xt[xb][:, s, hlf * 512:(hlf + 1) * 512], in0=pd[b][:], scalar=0.5,
                        in1=xt[xb][:, s, hlf * 512:(hlf + 1) * 512], op0=ALU.mult, op1=ALU.add),
                        reads=[("pd", b), ("xt", xb)], writes=[("xt", xb)])

        def store(t):
            b = t % 2
            dst = x_out[t * TT:(t + 1) * TT, :].rearrange("(s p) d -> p s d", p=128)
            sch.dma("sp", lambda e: e.dma_start(out=dst, in_=xt[b][:]), reads=[("xt", b)],
                    writes=[("xout", t)])

        load(0)
        norm_stats(0)
        norm_tr(0)
        for t in range(ntiles):
            if t + 1 < ntiles:
                load(t + 1)
            gate_up(t)
            if t + 1 < ntiles:
                norm_stats(t + 1)
            down(t)
            store(t)
            if t + 1 < ntiles:
                norm_tr(t + 1)

    ctx.phase(body)


CH = 256
DBG_SKIP = set()
NEG = -30000.0
GAMMAS = [float(np.float32(1.0) - np.float32(2.0) ** np.float32(-5.0 - h)) for h in range(4)]
POOL_W = (2, 4, 8, 16)
O_RQ, O_RK, O_RV, O_RG, O_PU, O_MQ, O_MK, O_MV = 0, 256, 512, 1024, 1536, 1792, 2048, 2304
CONST_SPECS = {
    "c_ident": ([128, 128], BF16), "c_cs": ([S, 256], F32), "c_xz": ([128, 2, 8], F32),
    "c_dm": ([128, 4, CH], BF16), "c_band": ([128, 12, 128], BF16), "c_onehot": ([32, S], BF16),
    "c_tri": ([128, 128], BF16),
}


def host_consts():
    import ml_dtypes
    bf = ml_dtypes.bfloat16
    f32 = np.float32
    c = {}
    c["c_ident"] = np.eye(128, dtype=f32).astype(bf)
    pos = np.arange(S, dtype=f32)
    angle = (1.0 / (f32(10000.0) ** np.linspace(0.0, 1.0, 32, dtype=f32))).astype(f32)
    angle = np.repeat(angle, 2)
    ang = (pos[:, None] * angle[None, :]).astype(f32)
    cos = np.cos(ang).astype(f32)
    sin = np.sin(ang).astype(f32)
    sgn = np.where(np.arange(64) % 2 == 0, -1.0, 1.0).astype(f32)
    sina = sin * sgn[None, :]
    c["c_cs"] = np.ascontiguousarray(
        np.concatenate([cos, cos * f32(0.125), sina, sina * f32(0.125)], axis=1).astype(f32))
    lg = np.log(np.array(GAMMAS, dtype=f32)).astype(f32)
    idx = np.arange(CH, dtype=f32)
    xi = np.exp(lg[None, :] * (idx[:, None] + 1.0)).astype(f32)
    zeta = np.exp(lg[None, :] * (CH - 1.0 - idx[:, None])).astype(f32)
    xz = np.concatenate([xi, zeta], axis=1).reshape(2, 128, 8).transpose(1, 0, 2)
    c["c_xz"] = np.ascontiguousarray(xz).astype(f32)
    m = np.arange(128, dtype=f32)[:, None]
    i = np.arange(CH, dtype=f32)[None, :]
    dm = np.zeros((128, 4, CH), dtype=f32)
    for h in range(4):
        dm[:, h, :] = np.where(i - m >= 0, np.exp(lg[h] * np.maximum(i - m, 0.0)), 0.0)
    c["c_dm"] = dm.astype(bf)
    band = np.zeros((128, 12, 128), dtype=f32)
    sidx = np.arange(128)[:, None]
    tidx = np.arange(128)[None, :]
    for gi, w in enumerate(POOL_W):
        inwin = ((sidx <= tidx) & (sidx > tidx - w)).astype(f32)
        eye = (sidx == tidx).astype(f32)
        cnt = np.minimum(tidx + 1, w).astype(f32)
        band[:, gi, :] = inwin / w - eye
        band[:, 4 + gi, :] = ((sidx - 128) > (tidx - w)).astype(f32) / w
        band[:, 8 + gi, :] = inwin / cnt - eye
    c["c_band"] = band.astype(bf)
    oh = np.zeros((32, S), dtype=f32)
    for j in range(S // CH):
        oh[j, j * CH:(j + 1) * CH] = 1.0
    c["c_onehot"] = oh.astype(bf)
    c["c_tri"] = np.where(sidx > tidx, NEG, 0.0).astype(f32).astype(bf)
    return c


def mixer_phase(ctx, x_in, x_out, g_ap, w_in_ap, pool_w_ap, pool_scale_ap, w_out_ap, cst, nchunks):
    NSUBT = nchunks * 2

    def body(sch, sb, pp):
        win = sb("win", [128, KD, 2560], BF16)
        wout = sb("wout", [128, KD, D], BF16)
        kte = sb("kte", [128, 4, nchunks * CH], BF16)
        vc = sb("vc", [128, NSUBT, 4, 65], BF16)
        gbc = sb("gbc", [128, D], F32)
        ident = sb("ident", [128, 128], BF16)
        tri = sb("tri", [128, 128], BF16)
        band = sb("band", [128, 12, 128], BF16)
        dm = sb("dm", [128, 4, CH], BF16)
        xz = sb("xz", [128, 2, 8], F32)
        pwbd = sb("pwbd", [128, 2, 128], BF16)
        pscale = sb("pscale", [128, 2], F32)
        epsb = sb("epsb", [128, 1], F32)
        xt = sb("xt", [128, 2, D], F32)
        ss = sb("ss", [128, 2], F32)
        rstd = sb("rstd", [128, 2], F32)
        hb0 = sb("hb0", [128, D], BF16)
        hb = [hb0, hb0]
        hT = sb("hT", [128, KD, CH], BF16)
        yT = sb("yT", [128, KD, CH], BF16)
        cs = sb("cs", [128, 2, 256], F32)
        t2 = sb("t2", [128, 512], F32)
        on = t2
        qkr = sb("qkr", [128, 512], BF16)
        yret = qkr
        qxz = [sb("qxz%d" % i, [128, 512], BF16) for i in range(2)]
        rT = sb("rT", [128, 6, CH], BF16)
        vret = [sb("vret%d" % i, [128, 512], BF16) for i in range(2)]
        sg = [sb("sg%d" % i, [128, 512], BF16) for i in range(2)]
        u = [sb("u%d" % i, [128, 256], BF16) for i in range(3)]
        inT = [sb("inT%d" % i, [128, 384], BF16) for i in range(2)]
        Rst = sb("Rst", [128, 2, 128], F32)
        Rb = sb("Rb", [128, 2, 128], BF16)
        st6 = sb("st6", [128, 8, 6], F32)
        mv = sb("mv", [128, 8, 2], F32)
        grs = sb("grs", [128, 8], F32)
        gnb = sb("gnb", [128, 8], F32)
        pooledT = sb("pooledT", [128, 2, CH], BF16)
        qte = sb("qte", [128, 4, CH], BF16)
        ksum = sb("ksum", [128, 4], F32)
        kmT = sb("kmT", [128, 4, 32], BF16)
        gsb = sb("gsb", [128, 4, 32], F32)
        top8 = sb("top8", [128, 4, 8], F32)
        bqp = sb("bqp", [128, 4, 96], BF16)
        PT = [sb("PT%d" % i, [128, 512], BF16) for i in range(2)]
        rden = sb("rden", [128, 4], F32)
        ym = sb("ym", [128, 256], BF16)
        tp = pp("tp", [128, 1024], BF16)
        pz = [pp("pz%d" % i, [128, 512]) for i in range(3)]
        ps = [pp("ps%d" % i, [128, 512]) for i in range(2)]
        po = [pp("po%d" % i, [128, 512]) for i in range(2)]
        tpb = ps[1][:].bitcast(BF16)
        pzc = [0]

        pz_fixed = [False]

        def nz():
            if pz_fixed[0]:
                return 2
            b = pzc[0] % 3
            pzc[0] += 1
            return b

        for k in range(KD):
            sch.dma("pool", lambda e, k=k: e.dma_start(out=win[:, k, :], in_=w_in_ap[k * 128:(k + 1) * 128, :]),
                    writes=[("win", k)])
        for k in range(KD):
            sch.dma("pool", lambda e, k=k: e.dma_start(out=wout[:, k, :], in_=w_out_ap[k * 128:(k + 1) * 128, :]),
                    writes=[("wout", k)])
        WIN = [("win", k) for k in range(KD)]
        WOUT = [("wout", k) for k in range(KD)]
        if "X1" not in DBG_SKIP:
            sch.op("pool", lambda e: e.memset(pwbd[:], 0.0), writes=["pwbd"])
            for pr in range(2):
                for a in range(2):
                    sch.dma("pool", lambda e, pr=pr, a=a: e.dma_start(
                        out=pwbd[a * 64:(a + 1) * 64, pr, a * 64:(a + 1) * 64], in_=pool_w_ap[2 * pr + a]),
                        reads=["pwbd"], writes=["pwbd"])
            sch.dma("sp", lambda e: e.dma_start(out=pscale[:], in_=pool_scale_ap.rearrange("(r p) -> p r", p=128),
                                                allow_slow_non_contiguous=True),
                    writes=["pscale"])
        sch.dma("sp", lambda e: e.dma_start(out=gbc[:], in_=g_ap.partition_broadcast(128)), writes=["gbc"])
        sch.dma("sp", lambda e: e.dma_start(out=ident[:], in_=cst["c_ident"]), writes=["ident"])
        if "X1" not in DBG_SKIP:
            sch.dma("sp", lambda e: e.dma_start(out=tri[:], in_=cst["c_tri"]), writes=["tri"])
            sch.dma("sp", lambda e: e.dma_start(out=band[:], in_=cst["c_band"]), writes=["band"])
            sch.dma("sp", lambda e: e.dma_start(out=dm[:], in_=cst["c_dm"]), writes=["dm"])
            sch.dma("sp", lambda e: e.dma_start(out=xz[:], in_=cst["c_xz"]), writes=["xz"])
            for h in range(4):
                sch.dma("sp", lambda e, h=h: e.dma_start(out=kte[64:96, h, :], in_=cst["c_onehot"][:, 0:nchunks * CH]),
                        writes=[("kte_oh", h)])
        sch.op("dve", lambda e: e.memset(epsb[:], EPS), writes=["epsb"])
        if "X1" not in DBG_SKIP:
            sch.op("pool", lambda e: e.memset(vc[:], 1.0), writes=["vc_init"])
            sch.op("pool", lambda e: e.memset(qte[:], 0.0), writes=["qte_init"])
            sch.op("pool", lambda e: e.memset(bqp[:], 0.0), writes=["bqp_init"])
            sch.op("pool", lambda e: e.memset(gsb[:], -1e30), writes=["gsb_init"])
            sch.op("pool", lambda e: e.memset(kmT[:], 0.0), writes=["kmT_init"])
            sch.op("pool", lambda e: e.memset(Rst[:], 0.0), writes=["Rst"])
            sch.op("pool", lambda e: e.memset(Rb[:], 0.0), writes=["Rb"])
            sch.op("pool", lambda e: e.memset(u[2][:], 0.0), writes=[("u", 2)])
        if DBG_SKIP:
            sch.op("pool", lambda e: e.memset(yT[:], 0.0), writes=["yT_init"])

        inplace = (x_in.tensor.name == x_out.tensor.name)
        if not inplace:
            sch.dma("sp", lambda e: e.dma_start(out=x_out[0:nchunks * CH, :], in_=x_in[0:nchunks * CH, :]),
                    writes=["xcopy"])

        def load_x(c):
            src = x_out[c * CH:(c + 1) * CH, :].rearrange("(s p) d -> p s d", p=128)
            sch.dma("sp", lambda e: e.dma_start(out=xt[:], in_=src), reads=["xcopy"], writes=["xt"])

        def norm_T(c):
            for s in range(2):
                sch.op("act", lambda e, s=s: e.activation(out=hb0[:], in_=xt[:, s, :], func=AF.Square,
                                                         accum_out=ss[:, s:s + 1]),
                       reads=["xt"], writes=[("hb", 0), ("ss", s)])
            sch.op("act", lambda e: e.activation(out=rstd[:], in_=ss[:], func=AF.Sqrt, bias=epsb[:],
                                                 scale=1.0 / D),
                   reads=[("ss", 0), ("ss", 1), "epsb"], writes=["rstd0"])
            sch.op("dve", lambda e: e.reciprocal(out=rstd[:], in_=rstd[:]), reads=["rstd0"], writes=["rstd"])
            for s in range(2):
                sch.op("dve", lambda e, s=s: e.scalar_tensor_tensor(
                    out=hb[s][:], in0=xt[:, s, :], scalar=rstd[:, s:s + 1], in1=gbc[:],
                    op0=ALU.mult, op1=ALU.mult),
                    reads=["xt", "rstd", "gbc"], writes=[("hb", 0)])

                def tr(e, s=s):
                    inst = None
                    for k in range(KD):
                        inst = e.transpose(out=tp[:, k * 128:(k + 1) * 128],
                                           in_=hb[s][:, k * 128:(k + 1) * 128], identity=ident[:])
                    return inst
                sch.op("pe", tr, reads=[("hb", 0), "ident"], writes=["tp"])
                sch.op("act", lambda e, s=s: e.copy(
                    out=hT[:, :, s * 128:(s + 1) * 128], in_=tp[:].rearrange("p (k t) -> p k t", k=KD)),
                    reads=["tp"], writes=[("hT", s)])

        def chunk(c):
            nonlocal sch
            tok0 = c * CH
            csrc = cst["c_cs"][tok0:tok0 + CH, :].rearrange("(s p) d -> p s d", p=128)
            sch.dma("sp", lambda e, csrc=csrc: e.dma_start(out=cs[:], in_=csrc), writes=["cs"])
            if c + 1 < nchunks:
                load_x(c + 1)

            def proj_tok(e, pb, s, col0, ncol, dst0=0):
                inst = None
                for k in range(KD):
                    inst = e.matmul(pz[pb][:, dst0:dst0 + ncol], lhsT=hT[:, k, s * 128:(s + 1) * 128],
                                    rhs=win[:, k, col0:col0 + ncol], start=(k == 0), stop=(k == KD - 1))
                return inst

            ucur = [(2 * c) % 3, (2 * c + 1) % 3]
            uprev = [(2 * c + 2) % 3, (2 * c) % 3]
            def p_v(s):
                b = nz()
                sch.op("pe", lambda e: proj_tok(e, b, s, O_RV, 512), reads=[("hT", s)] + WIN, writes=[("pz", b)])
                sch.op("act", lambda e: e.copy(out=vret[s][:], in_=pz[b][:]), reads=[("pz", b)],
                       writes=[("vret", s)])

            def p_g(s):
                b = nz()
                sch.op("pe", lambda e: proj_tok(e, b, s, O_RG, 512), reads=[("hT", s)] + WIN, writes=[("pz", b)])
                sch.op("act", lambda e: e.activation(out=sg[s][:], in_=pz[b][:], func=AF.Silu),
                       reads=[("pz", b)], writes=[("sg", s)])

            def p_uv(s):
                b = nz()

                def puv(e):
                    proj_tok(e, b, s, O_PU, 256, 0)
                    return proj_tok(e, b, s, O_MV, 256, 256)
                sch.op("pe", puv, reads=[("hT", s)] + WIN, writes=[("pz", b)])
                sch.op("dve", lambda e: e.tensor_copy(out=u[ucur[s]][:], in_=pz[b][:, 0:256]),
                       reads=[("pz", b)], writes=[("u", ucur[s])])
                sch.op("act", lambda e: e.copy(
                    out=vc[:, 2 * c + s, :, 0:64], in_=pz[b][:, 256:512].rearrange("p (h d) -> p h d", h=4)),
                    reads=[("pz", b), "vc_init"], writes=[("vc", 2 * c + s)])

            def p_qk(s):
                b = nz()
                sch.op("pe", lambda e: proj_tok(e, b, s, O_RQ, 512), reads=[("hT", s)] + WIN, writes=[("pz", b)])
                return b

            def rot(s, b):
                z4 = pz[b][:].rearrange("p (a h d) -> p a h d", a=2, h=4)
                cosb = cs[:, s, 0:128].rearrange("p (a o d) -> p a o d", a=2, o=1).broadcast_to([128, 2, 4, 64])
                z5 = pz[b][:].rearrange("p (a h d two) -> p a h d two", a=2, h=4, two=2)
                t25 = t2[:].rearrange("p (a h d two) -> p a h d two", a=2, h=4, two=2)
                sn5 = cs[:, s, 128:256].rearrange("p (a o d two) -> p a o d two", a=2, o=1, two=2)
                for par in range(2):
                    sch.op("dve", lambda e, par=par: e.tensor_tensor(
                        out=t25[:, :, :, :, par], in0=z5[:, :, :, :, 1 - par],
                        in1=sn5[:, :, :, :, par].broadcast_to([128, 2, 4, 32]), op=ALU.mult),
                        reads=[("pz", b), "cs"], writes=[("t2", par)])
                sch.op("dve", lambda e: e.tensor_tensor(out=z4, in0=z4, in1=cosb, op=ALU.mult),
                       reads=[("pz", b), "cs"], writes=[("pz", b)])
                sch.op("dve", lambda e: e.tensor_tensor(out=qkr[:], in0=pz[b][:], in1=t2[:], op=ALU.add),
                       reads=[("pz", b), ("t2", 0), ("t2", 1)], writes=["qkr"])
                xzb = xz[:, s, :].rearrange("p (a h o) -> p a h o", a=2, o=1).broadcast_to([128, 2, 4, 64])
                sch.op("dve", lambda e: e.tensor_tensor(
                    out=qxz[s][:].rearrange("p (a h d) -> p a h d", a=2, h=4),
                    in0=qkr[:].rearrange("p (a h d) -> p a h d", a=2, h=4), in1=xzb, op=ALU.mult),
                    reads=["qkr", "xz"], writes=[("qxz", s)])

            def tr_(s):
                def trr(e):
                    inst = None
                    srcs = [qkr[:, 0:128], qkr[:, 128:256], qxz[s][:, 0:128], qxz[s][:, 128:256],
                            qkr[:, 256:384], qkr[:, 384:512]]
                    for j, sr in enumerate(srcs):
                        inst = e.transpose(out=tp[:, j * 128:(j + 1) * 128], in_=sr, identity=ident[:])
                    return inst
                sch.op("pe", trr, reads=["qkr", ("qxz", s), "ident"], writes=["tp"])
                sch.op("act", lambda e: e.copy(
                    out=rT[:, :, s * 128:(s + 1) * 128],
                    in_=tp[:, 0:768].rearrange("p (j t) -> p j t", j=6)),
                    reads=["tp"], writes=[("rT", s)])

            bq0 = p_qk(0)
            bq1 = p_qk(1)
            rot(0, bq0)
            p_v(0)
            tr_(0)
            rot(1, bq1)
            p_g(0)
            p_uv(0)
            p_v(1)
            tr_(1)
            p_g(1)
            p_uv(1)

            if "X4" not in DBG_SKIP:
                for h in range(4):
                    b = nz()

                    def pqk(e, b=b, h=h):
                        inst = None
                        for k in range(KD):
                            e.matmul(pz[b][0:64, 0:256], lhsT=win[:, k, O_MQ + h * 64:O_MQ + (h + 1) * 64],
                                     rhs=hT[:, k, :], start=(k == 0), stop=(k == KD - 1))
                        for k in range(KD):
                            inst = e.matmul(pz[b][0:64, 256:512], lhsT=win[:, k, O_MK + h * 64:O_MK + (h + 1) * 64],
                                            rhs=hT[:, k, :], start=(k == 0), stop=(k == KD - 1))
                        return inst
                    sch.op("pe", pqk, reads=[("hT", 0), ("hT", 1)] + WIN, writes=[("pz", b)])
                    sch.op("dve", lambda e, b=b, h=h: e.tensor_scalar(
                        out=qte[0:64, h, :], in0=pz[b][0:64, 0:256], scalar1=0.125, scalar2=None, op0=ALU.mult),
                        reads=[("pz", b), "qte_init"], writes=[("qte_q", h)])
                    sch.op("act", lambda e, b=b, h=h: e.activation(
                        out=kte[0:64, h, tok0:tok0 + CH], in_=pz[b][0:64, 256:512], func=AF.Identity,
                        accum_out=ksum[0:64, h:h + 1]),
                        reads=[("pz", b)], writes=[("kte", h), ("ksum", h)])
                    sch.op("dve", lambda e, h=h: e.tensor_scalar(
                        out=kmT[0:64, h, c:c + 1], in0=ksum[0:64, h:h + 1], scalar1=1.0 / CH, scalar2=None,
                        op0=ALU.mult),
                        reads=[("ksum", h), "kmT_init"], writes=[("kmT", h)])

            real = sch
            recA = _Rec()
            sch = recA
            pz_fixed[0] = True
            if c + 1 < nchunks:
                norm_T(c + 1)

            if "ret" not in DBG_SKIP:
                for h in range(4):
                    pr, base = h // 2, 64 * (h % 2)
                    ib = h % 2

                    def inner(e, ib=ib, pr=pr, base=base):
                        e.matmul(pz[2][:, 0:256], lhsT=rT[base:base + 64, 4 + pr, 0:128],
                                 rhs=rT[base:base + 64, pr, 0:256], start=True, stop=True)
                        return e.matmul(pz[2][:, 256:384], lhsT=rT[base:base + 64, 4 + pr, 128:256],
                                        rhs=rT[base:base + 64, pr, 128:256], start=True, stop=True)
                    sch.op("pe", inner, reads=[("rT", 0), ("rT", 1)], writes=[("pz", 2)])
                    sch.op("dve", lambda e, h=h, ib=ib: e.tensor_tensor(
                        out=inT[ib][:, 0:256], in0=pz[2][:, 0:256], in1=dm[:, h, :], op=ALU.mult),
                        reads=[("pz", 2), "dm"], writes=[("inT", ib, 0)])
                    sch.op("dve", lambda e, h=h, ib=ib: e.tensor_tensor(
                        out=inT[ib][:, 256:384], in0=pz[2][:, 256:384], in1=dm[:, h, 0:128], op=ALU.mult),
                        reads=[("pz", 2), "dm"], writes=[("inT", ib, 1)])

                    def outm(e, h=h, pr=pr, base=base, ib=ib):
                        hc = slice(h * 128, (h + 1) * 128)
                        e.matmul(pz[0][:, hc], lhsT=inT[ib][:, 0:128], rhs=vret[0][:, hc], start=True,
                                 stop=False)
                        e.matmul(pz[0][:, hc], lhsT=rT[base:base + 64, 2 + pr, 0:128],
                                 rhs=Rb[base:base + 64, pr, :], start=False, stop=True)
                        e.matmul(pz[1][:, hc], lhsT=inT[ib][:, 128:256], rhs=vret[0][:, hc], start=True,
                                 stop=False)
                        e.matmul(pz[1][:, hc], lhsT=inT[ib][:, 256:384], rhs=vret[1][:, hc], start=False,
                                 stop=False)
                        return e.matmul(pz[1][:, hc], lhsT=rT[base:base + 64, 2 + pr, 128:256],
                                        rhs=Rb[base:base + 64, pr, :], start=False, stop=True)
                    sch.op("pe", outm, reads=[("inT", ib, 0), ("inT", ib, 1), ("vret", 0), ("vret", 1), ("rT", 0),
                                              ("rT", 1), "Rb"],
                           writes=[("pz", 0), ("pz", 1)])
                for s in range(2):
                    for h in range(4):
                        j = s * 4 + h
                        sch.op("dve", lambda e, s=s, h=h, j=j: e.bn_stats(
                            out=st6[:, j, :], in_=pz[s][:, h * 128:(h + 1) * 128]),
                            reads=[("pz", s)], writes=[("st6", j)])
                        sch.op("dve", lambda e, j=j: e.bn_aggr(out=mv[:, j, :], in_=st6[:, j, :]),
                               reads=[("st6", j)], writes=[("mv", j)])
                MV = [("mv", j) for j in range(8)]
                sch.op("act", lambda e: e.activation(out=grs[:], in_=mv[:, :, 1], func=AF.Sqrt, bias=epsb[:], scale=1.0),
                       reads=MV + ["epsb"], writes=["grs0"])
                sch.op("dve", lambda e: e.reciprocal(out=grs[:], in_=grs[:]), reads=["grs0"], writes=["grs"])
                sch.op("dve", lambda e: e.scalar_tensor_tensor(out=gnb[:], in0=mv[:, :, 0], scalar=-1.0, in1=grs[:],
                                                                op0=ALU.mult, op1=ALU.mult),
                       reads=MV + ["grs"], writes=["gnb"])
                for s in range(2):
                    for h in range(4):
                        j = s * 4 + h
                        sch.op("act", lambda e, s=s, h=h, j=j: e.activation(
                            out=on[:, h * 128:(h + 1) * 128], in_=pz[s][:, h * 128:(h + 1) * 128], func=AF.Identity,
                            bias=gnb[:, j:j + 1], scale=grs[:, j:j + 1]),
                            reads=[("pz", s), "grs", "gnb"], writes=[("on", h), ("t2", 0), ("t2", 1)])
                    sch.op("dve", lambda e, s=s: e.tensor_tensor(out=yret[:], in0=on[:], in1=sg[s][:], op=ALU.mult),
                           reads=[("on", h) for h in range(4)] + [("sg", s), ("t2", 0), ("t2", 1)], writes=["qkr"])

                    def try_(e):
                        inst = None
                        for k in range(4):
                            inst = e.transpose(out=tp[:, k * 128:(k + 1) * 128], in_=yret[:, k * 128:(k + 1) * 128],
                                               identity=ident[:])
                        return inst
                    sch.op("pe", try_, reads=["qkr", "ident"], writes=["tp"])
                    sch.op("act", lambda e, s=s: e.copy(
                        out=yT[:, 0:4, s * 128:(s + 1) * 128], in_=tp[:, 0:512].rearrange("p (k t) -> p k t", k=4)),
                        reads=["tp"], writes=[("yT", "ret", s)])
                for pr in range(2):
                    b = nz()

                    def stm(e, b=b, pr=pr):
                        inst = None
                        for a in range(2):
                            h = 2 * pr + a
                            for s in range(2):
                                inst = e.matmul(pz[b][:, a * 128:(a + 1) * 128],
                                                lhsT=qxz[s][:, 256 + pr * 128:256 + (pr + 1) * 128],
                                                rhs=vret[s][:, h * 128:(h + 1) * 128], start=(s == 0), stop=(s == 1))
                        return inst
                    sch.op("pe", stm, reads=[("qxz", 0), ("qxz", 1), ("vret", 0), ("vret", 1)], writes=[("pz", b)])
                    for a in range(2):
                        h = 2 * pr + a
                        gC = float(np.float32(GAMMAS[h]) ** np.float32(CH))
                        sch.op("dve", lambda e, b=b, pr=pr, a=a, gC=gC: e.scalar_tensor_tensor(
                            out=Rst[a * 64:(a + 1) * 64, pr, :], in0=Rst[a * 64:(a + 1) * 64, pr, :], scalar=gC,
                            in1=pz[b][a * 64:(a + 1) * 64, a * 128:(a + 1) * 128], op0=ALU.mult, op1=ALU.add),
                            reads=[("pz", b), "Rst"], writes=["Rst"])
                sch.op("act", lambda e: e.copy(out=Rb[:], in_=Rst[:]), reads=["Rst"], writes=["Rb"])

            if "pool" not in DBG_SKIP:
                for s in range(2):
                    for pr in range(2):
                        b = nz()

                        def poolm(e, b=b, s=s, pr=pr):
                            inst = None
                            first = (c == 0 and s == 0)
                            for a in range(2):
                                gi = 2 * pr + a
                                inst = e.matmul(pz[b][:, a * 128:(a + 1) * 128],
                                                lhsT=u[ucur[s]][:, pr * 128:(pr + 1) * 128],
                                                rhs=band[:, (8 + gi) if first else gi, :], start=True, stop=first)
                                if not first:
                                    inst = e.matmul(pz[b][:, a * 128:(a + 1) * 128],
                                                    lhsT=u[uprev[s]][:, pr * 128:(pr + 1) * 128],
                                                    rhs=band[:, 4 + gi, :], start=False, stop=True)
                            return inst
                        sch.op("pe", poolm, reads=[("u", ucur[s]), ("u", uprev[s]), "band"], writes=[("pz", b)])
                        sch.op("act", lambda e, b=b, s=s, pr=pr: e.copy(
                            out=pooledT[0:64, pr, s * 128:(s + 1) * 128], in_=pz[b][0:64, 0:128]),
                            reads=[("pz", b)], writes=[("pooledT", pr, s, 0)])
                        sch.op("dve", lambda e, b=b, s=s, pr=pr: e.tensor_copy(
                            out=pooledT[64:128, pr, s * 128:(s + 1) * 128], in_=pz[b][64:128, 128:256]),
                            reads=[("pz", b)], writes=[("pooledT", pr, s, 1)])
                for pr in range(2):
                    b = nz()
                    sch.op("pe", lambda e, b=b, pr=pr: e.matmul(pz[b][:, 0:256], lhsT=pwbd[:, pr, :],
                                                                rhs=pooledT[:, pr, :], start=True, stop=True),
                           reads=[("pooledT", pr, s, a) for s in range(2) for a in range(2)] + ["pwbd"],
                           writes=[("pz", b)])
                    sch.op("act", lambda e, b=b, pr=pr: e.activation(
                        out=yT[:, 4 + pr, :], in_=pz[b][:, 0:256], func=AF.Identity, scale=pscale[:, pr:pr + 1]),
                        reads=[("pz", b), "pscale"], writes=[("yT", "pool", pr)])

            recB = _Rec()
            sch = recB
            if "moba" not in DBG_SKIP:
                if c > 0:
                    for s in range(2):
                        def gatem(e, s=s):
                            inst = None
                            for h in range(4):
                                inst = e.matmul(ps[0][:, h * 32:(h + 1) * 32], lhsT=qte[0:64, h, s * 128:(s + 1) * 128],
                                                rhs=kmT[0:64, h, :], start=True, stop=True)
                            return inst
                        sch.op("pe", gatem, reads=[("qte_q", h) for h in range(4)] + [("kmT", h) for h in range(4)],
                               writes=[("ps", 0)])
                        sch.op("dve", lambda e: e.tensor_copy(
                            out=gsb[:, :, 0:c], in_=ps[0][:, 0:128].rearrange("p (h j) -> p h j", h=4)[:, :, 0:c]),
                            reads=[("ps", 0), "gsb_init"], writes=["gsb"])
                        for h in range(4):
                            sch.op("dve", lambda e, h=h: e.max(out=top8[:, h, :], in_=gsb[:, h, :]),
                                   reads=["gsb"], writes=[("top8", h)])
                            sch.op("dve", lambda e, h=h: e.tensor_scalar(
                                out=bqp[:, h, 64:64 + c], in0=gsb[:, h, 0:c], scalar1=top8[:, h, 2:3], scalar2=NEG,
                                op0=ALU.is_lt, op1=ALU.mult),
                                reads=["gsb", ("top8", h), "bqp_init"], writes=[("bqp", h)])

                        def trb(e):
                            inst = None
                            for h in range(4):
                                inst = e.transpose(out=tpb[0:96, h * 128:(h + 1) * 128], in_=bqp[:, h, :],
                                                   identity=ident[:])
                            return inst
                        sch.op("pe", trb, reads=[("bqp", h) for h in range(4)] + ["ident"], writes=[("ps", 1)])
                        sch.op("act", lambda e, s=s: e.copy(
                            out=qte[64:96, :, s * 128:(s + 1) * 128],
                            in_=tpb[64:96, 0:512].rearrange("p (h t) -> p h t", h=4)),
                            reads=[("ps", 1), "qte_init"], writes=[("qte_b", s)])
                QTE = [("qte_b", 0), ("qte_b", 1)]
                steps = [(h, j) for h in range(4) for j in range(c + 1)]

                def rec_qk(i):
                    h, j = steps[i]
                    own = (j == c)
                    sbk = i % 2

                    def qk(e):
                        k0 = j * CH
                        if not own:
                            e.matmul(ps[sbk][:, 0:256], lhsT=kte[0:96, h, k0:k0 + 128], rhs=qte[0:96, h, :],
                                     start=True, stop=True)
                            return e.matmul(ps[sbk][:, 256:512], lhsT=kte[0:96, h, k0 + 128:k0 + 256],
                                            rhs=qte[0:96, h, :], start=True, stop=True)
                        e.matmul(ps[sbk][:, 0:256], lhsT=kte[0:96, h, k0:k0 + 128], rhs=qte[0:96, h, :],
                                 start=True, stop=False)
                        e.matmul(ps[sbk][:, 0:128], lhsT=ident[:], rhs=tri[:], start=False, stop=True)
                        e.matmul(ps[sbk][:, 256:384], lhsT=kte[0:96, h, k0 + 128:k0 + 256],
                                 rhs=qte[0:96, h, 128:256], start=True, stop=False)
                        return e.matmul(ps[sbk][:, 256:384], lhsT=ident[:], rhs=tri[:], start=False, stop=True)
                    sch.op("pe", qk, reads=[("kte", h), ("kte_oh", h), ("qte_q", h), "ident", "tri", "qte_init"] + QTE,
                           writes=[("ps", sbk)])

                def rec_exp(i):
                    h, j = steps[i]
                    sbk = pbuf = i % 2
                    ncol = 384 if j == c else 512
                    sch.op("act", lambda e: e.activation(out=PT[pbuf][:, 0:ncol], in_=ps[sbk][:, 0:ncol], func=AF.Exp),
                           reads=[("ps", sbk)], writes=[("PT", pbuf)])

                def rec_pv(i):
                    h, j = steps[i]
                    own = (j == c)
                    pbuf = i % 2
                    hs = slice(h * 65, (h + 1) * 65)

                    def pv(e):
                        inst = None
                        first = (j == 0)
                        if not own:
                            for kc in range(2):
                                for s in range(2):
                                    inst = e.matmul(po[s][:, hs],
                                                    lhsT=PT[pbuf][:, kc * 256 + s * 128:kc * 256 + (s + 1) * 128],
                                                    rhs=vc[:, 2 * j + kc, h, :],
                                                    start=(first and kc == 0), stop=False)
                            return inst
                        e.matmul(po[0][:, hs], lhsT=PT[pbuf][:, 0:128], rhs=vc[:, 2 * j, h, :],
                                 start=first, stop=True)
                        e.matmul(po[1][:, hs], lhsT=PT[pbuf][:, 128:256], rhs=vc[:, 2 * j, h, :],
                                 start=first, stop=False)
                        return e.matmul(po[1][:, hs], lhsT=PT[pbuf][:, 256:384], rhs=vc[:, 2 * j + 1, h, :],
                                        start=False, stop=True)
                    sch.op("pe", pv, reads=[("PT", pbuf), ("vc", 2 * j), ("vc", 2 * j + 1)],
                           writes=[("po", 0), ("po", 1)])

                rec_qk(0)
                for i in range(len(steps)):
                    rec_exp(i)
                    if i + 1 < len(steps):
                        rec_qk(i + 1)
                    rec_pv(i)
                sch = real
                pz_fixed[0] = False
                _merge_play(real, recB.items, recA.items)
                for s in range(2):
                    po3 = po[s][:, 0:260].rearrange("p (h d) -> p h d", h=4)
                    sch.op("dve", lambda e, po3=po3: e.reciprocal(out=rden[:], in_=po3[:, :, 64]),
                           reads=[("po", s)], writes=["rden"])
                    sch.op("dve", lambda e, po3=po3: e.tensor_tensor(
                        out=ym[:].rearrange("p (h d) -> p h d", h=4), in0=po3[:, :, 0:64],
                        in1=rden[:].rearrange("p (h o) -> p h o", o=1).broadcast_to([128, 4, 64]), op=ALU.mult),
                        reads=[("po", s), "rden"], writes=["ym"])

                    def trm(e):
                        e.transpose(out=tp[:, 0:128], in_=ym[:, 0:128], identity=ident[:])
                        return e.transpose(out=tp[:, 128:256], in_=ym[:, 128:256], identity=ident[:])
                    sch.op("pe", trm, reads=["ym", "ident"], writes=["tp"])
                    sch.op("act", lambda e, s=s: e.copy(
                        out=yT[:, 6:8, s * 128:(s + 1) * 128], in_=tp[:, 0:256].rearrange("p (k t) -> p k t", k=2)),
                        reads=["tp"], writes=[("yT", "moba", s)])

            YT = [("yT", "ret", 0), ("yT", "ret", 1), ("yT", "pool", 0), ("yT", "pool", 1),
                  ("yT", "moba", 0), ("yT", "moba", 1)]
            for s in range(2):
                for hlf in range(2):
                    b = nz()

                    def wo(e, b=b, s=s, hlf=hlf):
                        inst = None
                        for k in range(KD):
                            inst = e.matmul(pz[b][:], lhsT=yT[:, k, s * 128:(s + 1) * 128],
                                            rhs=wout[:, k, hlf * 512:(hlf + 1) * 512], start=(k == 0),
                                            stop=(k == KD - 1))
                        return inst
                    sch.op("pe", wo, reads=YT + WOUT + ["yT_init"], writes=[("pz", b)])
                    sch.op("dve", lambda e, b=b, s=s, hlf=hlf: e.tensor_copy(
                        out=xt[:, s, hlf * 512:(hlf + 1) * 512], in_=pz[b][:]),
                        reads=[("pz", b), "xt"], writes=["xt"])
            dst = x_out[tok0:tok0 + CH, :].rearrange("(s p) d -> p s d", p=128)
            sch.dma("pool", lambda e, dst=dst: e.dma_start(out=dst, in_=xt[:], accum_op=ALU.add),
                    reads=["xt", "xcopy"], writes=[("xout", c)])

        load_x(0)
        norm_T(0)
        for c in range(nchunks):
            chunk(c)

    ctx.phase(body)


def ple_phase(ctx, x_in, x_out, p_ap, g_ap, wg_ap, wp_ap, ident_dram, gfin_ap, ntiles):
    final = gfin_ap is not None

    def body(sch, sb, pp):
        wg = sb("wg", [128, KD, D], BF16)
        wp = sb("wp", [128, 2, D], BF16)
        gbc = sb("gbc", [128, D], F32)
        gfb = sb("gfb", [128, D], F32)
        ident = sb("ident", [128, 128], BF16)
        epsb = sb("epsb", [128, 1], F32)
        xt = [sb("xt%d" % i, [128, NSUB, D], F32) for i in range(3)]
        pt = [sb("pt%d" % i, [128, NSUB, PLE], F32) for i in range(3)]
        sq = sb("sq", [128, D], BF16)
        ss = sb("ss", [128, NSUB], F32)
        rstd = sb("rstd", [128, NSUB], F32)
        ss2 = sb("ss2", [128, NSUB], F32)
        rstd2 = sb("rstd2", [128, NSUB], F32)
        hb = [sb("hb%d" % i, [128, D], BF16) for i in range(2)]
        pb16 = [sb("pb16_%d" % i, [128, PLE], BF16) for i in range(2)]
        hTs = [sb("hT%d" % i, [128, KD, TT], BF16) for i in range(2)]
        pTs = [sb("pT%d" % i, [128, 2, TT], BF16) for i in range(2)]
        sig = [sb("sig%d" % i, [128, 512], F32) for i in range(2)]
        tmp = [sb("tmp%d" % i, [128, 512], F32) for i in range(2)]
        ot = [sb("ot%d" % i, [128, NSUB, D], F32) for i in range(2)] if final else None
        tp = [pp("tp%d" % i, [128, D], BF16) for i in range(2)]
        tpp = pp("tpp", [128, 512], BF16)
        pg = [pp("pg%d" % i, [128, 512]) for i in range(2)]
        pj = [pp("pj%d" % i, [128, 512]) for i in range(2)]

        for k in range(KD):
            sch.dma("pool", lambda e, k=k: e.dma_start(out=wg[:, k, :], in_=wg_ap[k * 128:(k + 1) * 128, :]),
                    writes=[("wg", k)])
        for k in range(2):
            sch.dma("pool", lambda e, k=k: e.dma_start(out=wp[:, k, :], in_=wp_ap[k * 128:(k + 1) * 128, :]),
                    writes=[("wp", k)])
        sch.dma("sp", lambda e: e.dma_start(out=gbc[:], in_=g_ap.partition_broadcast(128)), writes=["gbc"])
        if final:
            sch.dma("sp", lambda e: e.dma_start(out=gfb[:], in_=gfin_ap.partition_broadcast(128)), writes=["gfb"])
        sch.dma("sp", lambda e: e.dma_start(out=ident[:], in_=ident_dram), writes=["ident"])
        sch.op("dve", lambda e: e.memset(epsb[:], EPS), writes=["epsb"])
        WG = [("wg", k) for k in range(KD)]
        WP = [("wp", k) for k in range(2)]

        def load(t):
            b = t % 3
            src = x_in[t * TT:(t + 1) * TT, :].rearrange("(s p) d -> p s d", p=128)
            sch.dma("sp", lambda e: e.dma_start(out=xt[b][:], in_=src), writes=[("xt", b)])
            psrc = p_ap[t * TT:(t + 1) * TT, :].rearrange("(s p) d -> p s d", p=128)
            sch.dma("sp", lambda e: e.dma_start(out=pt[b][:], in_=psrc), writes=[("pt", b)])

        def norm_stats(t):
            b = t % 3
            for s in range(NSUB):
                sch.op("act", lambda e, s=s: e.activation(out=sq[:], in_=xt[b][:, s, :], func=AF.Square,
                                                         accum_out=ss[:, s:s + 1]),
                       reads=[("xt", b)], writes=["sq", ("ss", s)])
            sch.op("act", lambda e: e.activation(out=rstd[:], in_=ss[:], func=AF.Sqrt, bias=epsb[:],
                                                 scale=1.0 / D),
                   reads=[("ss", s) for s in range(NSUB)] + ["epsb"], writes=["rstd0"])
            sch.op("dve", lambda e: e.reciprocal(out=rstd[:], in_=rstd[:]), reads=["rstd0"], writes=["rstd"])

        def norm_T(t, stats=True):
            b = t % 3
            hT = hTs[t % 2]
            pT = pTs[t % 2]
            hb2 = t % 2
            if stats:
                norm_stats(t)
            for s in range(NSUB):
                hbuf = hb[s % 2]
                tpb = tp[s % 2]
                pbuf = pb16[s % 2]
                sch.op("dve", lambda e, s=s, hbuf=hbuf: e.scalar_tensor_tensor(
                    out=hbuf[:], in0=xt[b][:, s, :], scalar=rstd[:, s:s + 1], in1=gbc[:],
                    op0=ALU.mult, op1=ALU.mult),
                    reads=[("xt", b), "rstd", "gbc"], writes=[("hb", s % 2)])
                sch.op("pool", lambda e, s=s, pbuf=pbuf: e.tensor_copy(out=pbuf[:], in_=pt[b][:, s, :]),
                       reads=[("pt", b)], writes=[("pb16", s % 2)])

                def tr(e, hbuf=hbuf, tpb=tpb):
                    inst = None
                    for k in range(KD):
                        inst = e.transpose(out=tpb[:, k * 128:(k + 1) * 128],
                                           in_=hbuf[:, k * 128:(k + 1) * 128], identity=ident[:])
                    return inst
                sch.op("pe", tr, reads=[("hb", s % 2), "ident"], writes=[("tp", s % 2)])
                sch.op("act", lambda e, s=s, tpb=tpb: e.copy(
                    out=hT[:, :, s * 128:(s + 1) * 128], in_=tpb[:].rearrange("p (k t) -> p k t", k=KD)),
                    reads=[("tp", s % 2)], writes=[("hT", hb2, s)])

                def trp(e, pbuf=pbuf):
                    e.transpose(out=tpp[:, 0:128], in_=pbuf[:, 0:128], identity=ident[:])
                    return e.transpose(out=tpp[:, 128:256], in_=pbuf[:, 128:256], identity=ident[:])
                sch.op("pe", trp, reads=[("pb16", s % 2), "ident"], writes=["tpp"])
                sch.op("act", lambda e, s=s: e.copy(
                    out=pT[:, :, s * 128:(s + 1) * 128], in_=tpp[:, 0:256].rearrange("p (k t) -> p k t", k=2)),
                    reads=["tpp"], writes=[("pT", hb2, s)])

        def compute(t):
            xb = t % 3
            hT = hTs[t % 2]
            pT = pTs[t % 2]
            hb2 = t % 2
            for s in range(NSUB):
                for hlf in range(2):
                    b = (s * 2 + hlf) % 2
                    cs_ = slice(hlf * 512, (hlf + 1) * 512)

                    def mm(e, s=s, b=b, cs_=cs_):
                        inst = None
                        for k in range(KD):
                            e.matmul(pg[b][:], lhsT=hT[:, k, s * 128:(s + 1) * 128], rhs=wg[:, k, cs_],
                                     start=(k == 0), stop=(k == KD - 1))
                        for k in range(2):
                            inst = e.matmul(pj[b][:], lhsT=pT[:, k, s * 128:(s + 1) * 128], rhs=wp[:, k, cs_],
                                            start=(k == 0), stop=(k == 1))
                        return inst
                    sch.op("pe", mm, reads=[("hT", hb2, s), ("pT", hb2, s)] + WG + WP, writes=[("pg", b), ("pj", b)])
                    sch.op("act", lambda e, b=b: e.activation(out=sig[b][:], in_=pg[b][:], func=AF.Sigmoid),
                           reads=[("pg", b)], writes=[("sig", b)])
                    sch.op("dve", lambda e, b=b: e.tensor_tensor(out=tmp[b][:], in0=pj[b][:], in1=sig[b][:],
                                                                  op=ALU.mult),
                           reads=[("pj", b), ("sig", b)], writes=[("tmp", b)])
                    sch.op("pool", lambda e, b=b, s=s, cs_=cs_: e.tensor_tensor(
                        out=xt[xb][:, s, cs_], in0=xt[xb][:, s, cs_], in1=tmp[b][:], op=ALU.add),
                        reads=[("tmp", b), ("xt", xb)], writes=[("xt", xb)])
            if final:
                for s in range(NSUB):
                    sch.op("act", lambda e, s=s: e.activation(out=sq[:], in_=xt[xb][:, s, :], func=AF.Square,
                                                             accum_out=ss2[:, s:s + 1]),
                           reads=[("xt", xb)], writes=["sq", ("ss2", s)])
                sch.op("act", lambda e: e.activation(out=rstd2[:], in_=ss2[:], func=AF.Sqrt, bias=epsb[:],
                                                     scale=1.0 / D),
                       reads=[("ss2", s) for s in range(NSUB)] + ["epsb"], writes=["rstd20"])
                sch.op("dve", lambda e: e.reciprocal(out=rstd2[:], in_=rstd2[:]), reads=["rstd20"],
                       writes=["rstd2"])
                for s in range(NSUB):
                    sch.op("dve", lambda e, s=s: e.scalar_tensor_tensor(
                        out=ot[t % 2][:, s, :], in0=xt[xb][:, s, :], scalar=rstd2[:, s:s + 1], in1=gfb[:],
                        op0=ALU.mult, op1=ALU.mult),
                        reads=[("xt", xb), "rstd2", "gfb"], writes=[("ot", t % 2)])

        def store(t):
            b = t % 3
            dst = x_out[t * TT:(t + 1) * TT, :].rearrange("(s p) d -> p s d", p=128)
            if final:
                sch.dma("sp", lambda e: e.dma_start(out=dst, in_=ot[t % 2][:]), reads=[("ot", t % 2)],
                        writes=[("xout", t)])
            else:
                sch.dma("sp", lambda e: e.dma_start(out=dst, in_=xt[b][:]), reads=[("xt", b)],
                        writes=[("xout", t)])

        def run():
            nonlocal sch
            real = sch
            load(0)
            if ntiles > 1:
                load(1)
            norm_T(0)
            for t in range(ntiles):
                if t + 2 < ntiles:
                    load(t + 2)
                ra = _Rec()
                sch = ra
                compute(t)
                store(t)
                rb = _Rec()
                sch = rb
                if t + 1 < ntiles:
                    norm_T(t + 1)
                sch = real
                _merge_play(real, ra.items, rb.items)

        run()

    ctx.phase(body)


WEIGHT_SPECS = {
    "norm_ffn1": [DEPTH, D], "ffn1_w_gate": [DEPTH, D, DFF], "ffn1_w_up": [DEPTH, D, DFF],
    "ffn1_w_down": [DEPTH, DFF, D], "norm_mix": [DEPTH, D], "w_in": [DEPTH, D, 2560],
    "pool_w": [DEPTH, 4, 64, 64], "pool_scale": [DEPTH, 256], "w_out": [DEPTH, D, D],
    "norm_ffn2": [DEPTH, D], "ffn2_w_gate": [DEPTH, D, DFF], "ffn2_w_up": [DEPTH, D, DFF],
    "ffn2_w_down": [DEPTH, DFF, D], "norm_ple": [DEPTH, D], "ple_w_gate": [DEPTH, D, D],
    "ple_w_proj": [DEPTH, PLE, D], "norm_final": [D],
}


def build_nc(ntok=S, phases=None, depth=DEPTH):
    nc = bass.Bass("TRN2", target_bir_lowering=False)
    x = nc.dram_tensor("x", [ntok, D], F32, kind="ExternalInput").ap()
    p = nc.dram_tensor("p", [DEPTH, ntok, PLE], F32, kind="ExternalInput").ap()
    w = {k: nc.dram_tensor(k, shp, F32, kind="ExternalInput").ap() for k, shp in WEIGHT_SPECS.items()}
    cst = {}
    for k, (shp, dt) in CONST_SPECS.items():
        cst[k] = nc.dram_tensor(k, shp, dt, kind="ExternalInput").ap()
    out = nc.dram_tensor("out", [ntok, D], F32, kind="ExternalOutput").ap()
    xs = nc.dram_tensor("xs", [ntok, D], F32).ap()
    ident = cst["c_ident"]
    plan = []
    for i in range(depth):
        plan += [("ffn1", i), ("mix", i), ("ffn2", i), ("ple", i)]
    if phases is not None:
        plan = phases
    with contextlib.ExitStack() as stack:
        ctx = Ctx(nc, stack)
        cur = x
        for n, (ph, i) in enumerate(plan):
            last = (n == len(plan) - 1)
            dst = out if last else xs
            if ph == "ffn1":
                ffn_phase(ctx, "f1", cur, dst, w["norm_ffn1"][i], w["ffn1_w_gate"][i], w["ffn1_w_up"][i],
                          w["ffn1_w_down"][i], ident, ntok // TT)
            elif ph == "ffn2":
                ffn_phase(ctx, "f2", cur, dst, w["norm_ffn2"][i], w["ffn2_w_gate"][i], w["ffn2_w_up"][i],
                          w["ffn2_w_down"][i], ident, ntok // TT)
            elif ph == "mix":
                mixer_phase(ctx, cur, dst, w["norm_mix"][i], w["w_in"][i], w["pool_w"][i], w["pool_scale"][i],
                            w["w_out"][i], cst, ntok // CH)
            elif ph == "ple":
                fin = w["norm_final"] if (last and phases is None) else None
                ple_phase(ctx, cur, dst, p[i], w["norm_ple"][i], w["ple_w_gate"][i], w["ple_w_proj"][i], ident,
                          fin, ntok // TT)
            elif ph == "plefin":
                ple_phase(ctx, cur, dst, p[i], w["norm_ple"][i], w["ple_w_gate"][i], w["ple_w_proj"][i], ident,
                          w["norm_final"], ntok // TT)
            cur = dst
    return nc


_CONSTS = None


def kernel(**inputs):
    global _CONSTS
    if _CONSTS is None:
        _CONSTS = host_consts()
    x = np.ascontiguousarray(np.asarray(inputs["x"], dtype=np.float32))
    p = np.ascontiguousarray(np.asarray(inputs["p"], dtype=np.float32))
    shared = {k: np.ascontiguousarray(np.asarray(inputs[k], dtype=np.float32)) for k in WEIGHT_SPECS}
    shared.update(_CONSTS)
    nc = build_nc()
    in_maps = []
    for b in range(NCORES):
        m = dict(shared)
        m["x"] = x[b]
        m["p"] = np.ascontiguousarray(p[:, b])
        in_maps.append(m)
    res = run_bass_kernel_spmd(nc, in_maps, core_ids=list(range(NCORES)))
    return np.stack([np.asarray(r["out"], dtype=np.float32) for r in res.results], axis=0)
```

```python
import contextlib
import numpy as np
import concourse.bass as bass
import concourse.mybir as mybir
from concourse.bass_utils import run_bass_kernel_spmd

F32 = mybir.dt.float32
BF16 = mybir.dt.bfloat16
AF = mybir.ActivationFunctionType
ALU = mybir.AluOpType
AX = mybir.AxisListType

D = 1024
S = 8192
NCORES = 8
DEPTH = 2
DFF = 2816
NF = DFF // 128
KD = D // 128
PLE = 256
EPS = 1e-6
TT = 512
NSUB = TT // 128


class _Op:
    __slots__ = ("eng", "fn", "deps", "is_dma", "needs_inc", "ordinal", "idx", "dma_n")

    def __init__(self, eng, fn, is_dma):
        self.eng = eng
        self.fn = fn
        self.deps = []
        self.is_dma = is_dma
        self.needs_inc = False
        self.ordinal = None
        self.idx = None
        self.dma_n = None


ENGINES = ("pe", "act", "dve", "pool", "sp")
PSUM_KEYS = {"pz", "ps", "po", "tp", "tpp", "pg", "pu", "pd", "pj"}
DMA_K = 8


class Sched:
    def __init__(self, nc, sems, dma_sems, counts, dma_counts):
        self.nc = nc
        self.sems = sems
        self.dma_sems = dma_sems
        self.counts = counts
        self.dma_counts = dma_counts
        self.ops = []
        self.last_w = {}
        self.readers = {}

    def _add(self, eng, fn, reads, writes, is_dma):
        ps_r = [r for r in reads if (r[0] if isinstance(r, tuple) else r) in PSUM_KEYS]
        if ps_r:
            reads = [r for r in reads if r not in ps_r]
            writes = list(writes) + [r for r in ps_r if r not in writes]
        op = _Op(eng, fn, is_dma)
        op.idx = len(self.ops)
        deps = {}
        for r in reads:
            w = self.last_w.get(r)
            if w is not None:
                deps[w.idx] = w
        for wkey in writes:
            w = self.last_w.get(wkey)
            if w is not None:
                deps[w.idx] = w
            for rd in self.readers.get(wkey, {}).values():
                deps[rd.idx] = rd
        deps.pop(op.idx, None)
        for d in deps.values():
            if (not d.is_dma) and (not is_dma) and d.eng == "pe" and eng == "pe":
                continue
            op.deps.append(d)
            d.needs_inc = True
        for r in reads:
            self.readers.setdefault(r, {})[(eng, is_dma)] = op
        for wkey in writes:
            self.last_w[wkey] = op
            self.readers[wkey] = {}
        self.ops.append(op)
        return op

    def op(self, eng, fn, reads=(), writes=()):
        return self._add(eng, fn, reads, writes, False)

    def dma(self, eng, fn, reads=(), writes=()):
        return self._add(eng, fn, reads, writes, True)

    def _token(self, d):
        if d.is_dma:
            n = d.dma_n
            return self.dma_sems[d.eng][n % DMA_K], 16 * (n // DMA_K + 1)
        return self.sems[d.eng], d.ordinal

    def emit(self, block, waited):
        for op in self.ops:
            if op.is_dma:
                op.dma_n = self.dma_counts[op.eng]
                self.dma_counts[op.eng] += 1
            elif op.needs_inc:
                self.counts[op.eng] += 1
                op.ordinal = self.counts[op.eng]
        per_eng = {e: [o for o in self.ops if o.eng == e] for e in ENGINES}
        final_tokens = []
        for e in ENGINES:
            last = None
            for o in reversed(per_eng[e]):
                if not o.is_dma:
                    last = o
                    break
            if last is not None:
                if not last.needs_inc:
                    last.needs_inc = True
                    self.counts[e] += 1
                    last.ordinal = self.counts[e]
                final_tokens.append((self.sems[e], last.ordinal))
            dmas = [o for o in per_eng[e] if o.is_dma]
            for o in dmas[-DMA_K:]:
                final_tokens.append(self._token(o))

        def run(e, eng_obj):
            wd = waited[e]
            for op in per_eng[e]:
                for d in op.deps:
                    sem, val = self._token(d)
                    if wd.get(sem.num, 0) < val:
                        eng_obj.wait_ge(sem, val)
                        wd[sem.num] = val
                if op.is_dma and op.dma_n >= DMA_K:
                    sem = self.dma_sems[e][op.dma_n % DMA_K]
                    val = 16 * (op.dma_n // DMA_K)
                    if wd.get(sem.num, 0) < val:
                        eng_obj.wait_ge(sem, val)
                        wd[sem.num] = val
                inst = op.fn(eng_obj)
                if op.is_dma:
                    inst.then_inc(self.dma_sems[e][op.dma_n % DMA_K], 16)
                elif op.needs_inc:
                    inst.then_inc(self.sems[e], 1)
            for sem, val in final_tokens:
                if wd.get(sem.num, 0) < val:
                    eng_obj.wait_ge(sem, val)
                    wd[sem.num] = val

        @block.tensor
        def _(eng):
            run("pe", eng)

        @block.scalar
        def _(eng):
            run("act", eng)

        @block.vector
        def _(eng):
            run("dve", eng)

        @block.gpsimd
        def _(eng):
            run("pool", eng)

        @block.sync
        def _(eng):
            run("sp", eng)


class _Rec:
    def __init__(self):
        self.items = []

    def op(self, eng, fn, reads=(), writes=()):
        self.items.append((False, eng, fn, reads, writes))

    def dma(self, eng, fn, reads=(), writes=()):
        self.items.append((True, eng, fn, reads, writes))


def _merge_play(sch, a, b):
    na, nb = len(a), len(b)
    ia = ib = 0
    while ia < na or ib < nb:
        if ib >= nb or (ia < na and ia * nb <= ib * na):
            it = a[ia]
            ia += 1
        else:
            it = b[ib]
            ib += 1
        (sch.dma if it[0] else sch.op)(it[1], it[2], it[3], it[4])


class Ctx:
    def __init__(self, nc, stack):
        self.nc = nc
        self.stack = stack
        self.sems = {e: stack.enter_context(nc.semaphore("s_" + e)) for e in ENGINES}
        self.dma_sems = {
            e: [stack.enter_context(nc.semaphore("d_%s%d" % (e, i))) for i in range(DMA_K)]
            for e in ("sp", "pool", "act")
        }
        self.counts = {e: 0 for e in ENGINES}
        self.dma_counts = {e: 0 for e in ("sp", "pool", "act")}
        self.waited = {e: {} for e in ENGINES}

    def phase(self, body):
        nc = self.nc
        self.nphase = getattr(self, "nphase", 0) + 1
        pfx = "p%d_" % self.nphase
        with contextlib.ExitStack() as ps:
            def sb(name, shape, dt):
                return ps.enter_context(nc.sbuf_tensor(pfx + name, list(shape), dt))

            def pp(name, shape, dt=F32):
                return ps.enter_context(nc.psum_tensor(pfx + name, list(shape), dt))

            sch = Sched(nc, self.sems, self.dma_sems, self.counts, self.dma_counts)
            body(sch, sb, pp)
            with nc.Block() as block:
                sch.emit(block, self.waited)


def ffn_phase(ctx, tag, x_in, x_out, g_ap, wg_ap, wu_ap, wd_ap, ident_dram, ntiles):
    def body(sch, sb, pp):
        wg = sb("wg", [128, KD, DFF], BF16)
        wu = sb("wu", [128, KD, DFF], BF16)
        wd = sb("wd", [128, NF, D], BF16)
        gbc = sb("gbc", [128, D], F32)
        ident = sb("ident", [128, 128], BF16)
        epsb = sb("epsb", [128, 1], F32)
        xt = [sb("xt%d" % i, [128, NSUB, D], F32) for i in range(2)]
        sq = sb("sq", [128, D], BF16)
        ss = sb("ss", [128, NSUB], F32)
        rstd = sb("rstd", [128, NSUB], F32)
        hb = [sb("hb%d" % i, [128, D], BF16) for i in range(2)]
        hT = sb("hT", [128, KD, TT], BF16)
        sil = [sb("sil%d" % i, [128, TT], BF16) for i in range(2)]
        act = sb("act", [128, NF, TT], BF16)
        tp = [pp("tp%d" % i, [128, D], BF16) for i in range(2)]
        pg = [pp("pg%d" % i, [128, TT]) for i in range(2)]
        pu = [pp("pu%d" % i, [128, TT]) for i in range(2)]
        pd = [pp("pd%d" % i, [128, 512]) for i in range(2)]

        CB = [(0, 3), (3, 8), (8, 15), (15, 22)]
        for cb, (f0, f1) in enumerate(CB):
            for (wt, wap, nm) in ((wg, wg_ap, "wg"), (wu, wu_ap, "wu")):
                sch.dma("pool", lambda e, wt=wt, wap=wap, f0=f0, f1=f1: e.dma_start(
                    out=wt[:, :, f0 * 128:f1 * 128],
                    in_=wap[:, f0 * 128:f1 * 128].rearrange("(k p) f -> p k f", p=128)),
                    writes=[(nm, cb)])

        def cb_of(f):
            for cb, (f0, f1) in enumerate(CB):
                if f0 <= f < f1:
                    return cb
        for f in range(NF):
            sch.dma("pool", lambda e, f=f: e.dma_start(out=wd[:, f, :], in_=wd_ap[f * 128:(f + 1) * 128, :]),
                    writes=[("wd", f)])
        sch.dma("sp", lambda e: e.dma_start(out=gbc[:], in_=g_ap.partition_broadcast(128)), writes=["gbc"])
        sch.dma("sp", lambda e: e.dma_start(out=ident[:], in_=ident_dram), writes=["ident"])
        sch.op("dve", lambda e: e.memset(epsb[:], EPS), writes=["epsb"])

        def load(t):
            b = t % 2
            src = x_in[t * TT:(t + 1) * TT, :].rearrange("(s p) d -> p s d", p=128)
            sch.dma("sp", lambda e: e.dma_start(out=xt[b][:], in_=src), writes=[("xt", b)])

        def norm_stats(t):
            b = t % 2
            for s in range(NSUB):
                sch.op("act", lambda e, s=s: e.activation(out=sq[:], in_=xt[b][:, s, :], func=AF.Square,
                                                         accum_out=ss[:, s:s + 1]),
                       reads=[("xt", b)], writes=["sq", ("ss", s)])
            sch.op("act", lambda e: e.activation(out=rstd[:], in_=ss[:], func=AF.Sqrt, bias=epsb[:],
                                                 scale=1.0 / D),
                   reads=[("ss", s) for s in range(NSUB)] + ["epsb"], writes=["rstd0"])
            sch.op("dve", lambda e: e.reciprocal(out=rstd[:], in_=rstd[:]), reads=["rstd0"], writes=["rstd"])
            for s in range(2):
                norm_h(t, s)

        def norm_h(t, s):
            b = t % 2
            hbuf = hb[s % 2]
            sch.op("dve", lambda e: e.scalar_tensor_tensor(
                out=hbuf[:], in0=xt[b][:, s, :], scalar=rstd[:, s:s + 1], in1=gbc[:],
                op0=ALU.mult, op1=ALU.mult),
                reads=[("xt", b), "rstd", "gbc"], writes=[("hb", s % 2)])

        def norm_tr(t):
            for s in range(NSUB):
                if s >= 2:
                    norm_h(t, s)
                hbuf = hb[s % 2]
                tpb = tp[s % 2]

                def tr(e, hbuf=hbuf, tpb=tpb):
                    inst = None
                    for k in range(KD):
                        inst = e.transpose(out=tpb[:, k * 128:(k + 1) * 128],
                                           in_=hbuf[:, k * 128:(k + 1) * 128], identity=ident[:])
                    return inst
                sch.op("pe", tr, reads=[("hb", s % 2), "ident"], writes=[("tp", s % 2)])
                sch.op("act", lambda e, s=s, tpb=tpb: e.copy(
                    out=hT[:, :, s * 128:(s + 1) * 128], in_=tpb[:].rearrange("p (k t) -> p k t", k=KD)),
                    reads=[("tp", s % 2)], writes=[("hT", s)])

        def gate_up(t):
            for f in range(NF):
                b = f % 2

                def mm(e, f=f, b=b):
                    inst = None
                    for k in range(KD):
                        e.matmul(pg[b][:], lhsT=wg[:, k, f * 128:(f + 1) * 128], rhs=hT[:, k, :],
                                 start=(k == 0), stop=(k == KD - 1))
                    for k in range(KD):
                        inst = e.matmul(pu[b][:], lhsT=wu[:, k, f * 128:(f + 1) * 128], rhs=hT[:, k, :],
                                        start=(k == 0), stop=(k == KD - 1))
                    return inst
                sch.op("pe", mm,
                       reads=[("wg", cb_of(f)), ("wu", cb_of(f))] + [("hT", s) for s in range(NSUB)],
                       writes=[("pg", b), ("pu", b)])
                sch.op("act", lambda e, b=b: e.activation(out=sil[b][:], in_=pg[b][:], func=AF.Silu),
                       reads=[("pg", b)], writes=[("sil", b)])
                sch.op("dve", lambda e, f=f, b=b: e.tensor_tensor(out=act[:, f, :], in0=pu[b][:], in1=sil[b][:],
                                                                 op=ALU.mult),
                       reads=[("pu", b), ("sil", b)], writes=[("act", f)])

        def down(t):
            xb = t % 2
            for s in range(NSUB):
                for hlf in range(2):
                    b = (s * 2 + hlf) % 2

                    def mm(e, s=s, hlf=hlf, b=b):
                        inst = None
                        for f in range(NF):
                            inst = e.matmul(pd[b][:], lhsT=act[:, f, s * 128:(s + 1) * 128],
                                            rhs=wd[:, f, hlf * 512:(hlf + 1) * 512],
                                            start=(f == 0), stop=(f == NF - 1))
                        return inst
                    sch.op("pe", mm, reads=[("act", f) for f in range(NF)] + [("wd", f) for f in range(NF)],
                           writes=[("pd", b)])
                    sch.op("dve", lambda e, s=s, hlf=hlf, b=b: e.scalar_tensor_tensor(
                        out=xt[xb][:, s, hlf * 512:(hlf + 1) * 512], in0=pd[b][:], scalar=0.5,
                        in1=xt[xb][:, s, hlf * 512:(hlf + 1) * 512], op0=ALU.mult, op1=ALU.add),
                        reads=[("pd", b), ("xt", xb)], writes=[("xt", xb)])

        def store(t):
            b = t % 2
            dst = x_out[t * TT:(t + 1) * TT, :].rearrange("(s p) d -> p s d", p=128)
            sch.dma("sp", lambda e: e.dma_start(out=dst, in_=xt[b][:]), reads=[("xt", b)],
                    writes=[("xout", t)])

        load(0)
        norm_stats(0)
        norm_tr(0)
        for t in range(ntiles):
            if t + 1 < ntiles:
                load(t + 1)
            gate_up(t)
            if t + 1 < ntiles:
                norm_stats(t + 1)
            down(t)
            store(t)
            if t + 1 < ntiles:
                norm_tr(t + 1)

    ctx.phase(body)


CH = 256
DBG_SKIP = set()
NEG = -30000.0
GAMMAS = [float(np.float32(1.0) - np.float32(2.0) ** np.float32(-5.0 - h)) for h in range(4)]
POOL_W = (2, 4, 8, 16)
O_RQ, O_RK, O_RV, O_RG, O_PU, O_MQ, O_MK, O_MV = 0, 256, 512, 1024, 1536, 1792, 2048, 2304
CONST_SPECS = {
    "c_ident": ([128, 128], BF16), "c_cs": ([S, 256], F32), "c_xz": ([128, 2, 8], F32),
    "c_dm": ([128, 4, CH], BF16), "c_band": ([128, 12, 128], BF16), "c_onehot": ([32, S], BF16),
    "c_tri": ([128, 128], BF16),
}


def host_consts():
    import ml_dtypes
    bf = ml_dtypes.bfloat16
    f32 = np.float32
    c = {}
    c["c_ident"] = np.eye(128, dtype=f32).astype(bf)
    pos = np.arange(S, dtype=f32)
    angle = (1.0 / (f32(10000.0) ** np.linspace(0.0, 1.0, 32, dtype=f32))).astype(f32)
    angle = np.repeat(angle, 2)
    ang = (pos[:, None] * angle[None, :]).astype(f32)
    cos = np.cos(ang).astype(f32)
    sin = np.sin(ang).astype(f32)
    sgn = np.where(np.arange(64) % 2 == 0, -1.0, 1.0).astype(f32)
    sina = sin * sgn[None, :]
    c["c_cs"] = np.ascontiguousarray(
        np.concatenate([cos, cos * f32(0.125), sina, sina * f32(0.125)], axis=1).astype(f32))
    lg = np.log(np.array(GAMMAS, dtype=f32)).astype(f32)
    idx = np.arange(CH, dtype=f32)
    xi = np.exp(lg[None, :] * (idx[:, None] + 1.0)).astype(f32)
    zeta = np.exp(lg[None, :] * (CH - 1.0 - idx[:, None])).astype(f32)
    xz = np.concatenate([xi, zeta], axis=1).reshape(2, 128, 8).transpose(1, 0, 2)
    c["c_xz"] = np.ascontiguousarray(xz).astype(f32)
    m = np.arange(128, dtype=f32)[:, None]
    i = np.arange(CH, dtype=f32)[None, :]
    dm = np.zeros((128, 4, CH), dtype=f32)
    for h in range(4):
        dm[:, h, :] = np.where(i - m >= 0, np.exp(lg[h] * np.maximum(i - m, 0.0)), 0.0)
    c["c_dm"] = dm.astype(bf)
    band = np.zeros((128, 12, 128), dtype=f32)
    sidx = np.arange(128)[:, None]
    tidx = np.arange(128)[None, :]
    for gi, w in enumerate(POOL_W):
        inwin = ((sidx <= tidx) & (sidx > tidx - w)).astype(f32)
        eye = (sidx == tidx).astype(f32)
        cnt = np.minimum(tidx + 1, w).astype(f32)
        band[:, gi, :] = inwin / w - eye
        band[:, 4 + gi, :] = ((sidx - 128) > (tidx - w)).astype(f32) / w
        band[:, 8 + gi, :] = inwin / cnt - eye
    c["c_band"] = band.astype(bf)
    oh = np.zeros((32, S), dtype=f32)
    for j in range(S // CH):
        oh[j, j * CH:(j + 1) * CH] = 1.0
    c["c_onehot"] = oh.astype(bf)
    c["c_tri"] = np.where(sidx > tidx, NEG, 0.0).astype(f32).astype(bf)
    return c


def mixer_phase(ctx, x_in, x_out, g_ap, w_in_ap, pool_w_ap, pool_scale_ap, w_out_ap, cst, nchunks):
    NSUBT = nchunks * 2

    def body(sch, sb, pp):
        win = sb("win", [128, KD, 2560], BF16)
        wout = sb("wout", [128, KD, D], BF16)
        kte = sb("kte", [128, 4, nchunks * CH], BF16)
        vc = sb("vc", [128, NSUBT, 4, 65], BF16)
        gbc = sb("gbc", [128, D], F32)
        ident = sb("ident", [128, 128], BF16)
        tri = sb("tri", [128, 128], BF16)
        band = sb("band", [128, 12, 128], BF16)
        dm = sb("dm", [128, 4, CH], BF16)
        xz = sb("xz", [128, 2, 8], F32)
        pwbd = sb("pwbd", [128, 2, 128], BF16)
        pscale = sb("pscale", [128, 2], F32)
        epsb = sb("epsb", [128, 1], F32)
        xt = sb("xt", [128, 2, D], F32)
        ss = sb("ss", [128, 2], F32)
        rstd = sb("rstd", [128, 2], F32)
        hb0 = sb("hb0", [128, D], BF16)
        hb = [hb0, hb0]
        hT = sb("hT", [128, KD, CH], BF16)
        yT = sb("yT", [128, KD, CH], BF16)
        cs = sb("cs", [128, 2, 256], F32)
        t2 = sb("t2", [128, 512], F32)
        on = t2
        qkr = sb("qkr", [128, 512], BF16)
        yret = qkr
        qxz = [sb("qxz%d" % i, [128, 512], BF16) for i in range(2)]
        rT = sb("rT", [128, 6, CH], BF16)
        vret = [sb("vret%d" % i, [128, 512], BF16) for i in range(2)]
        sg = [sb("sg%d" % i, [128, 512], BF16) for i in range(2)]
        u = [sb("u%d" % i, [128, 256], BF16) for i in range(3)]
        inT = [sb("inT%d" % i, [128, 384], BF16) for i in range(2)]
        Rst = sb("Rst", [128, 2, 128], F32)
        Rb = sb("Rb", [128, 2, 128], BF16)
        st6 = sb("st6", [128, 8, 6], F32)
        mv = sb("mv", [128, 8, 2], F32)
        grs = sb("grs", [128, 8], F32)
        gnb = sb("gnb", [128, 8], F32)
        pooledT = sb("pooledT", [128, 2, CH], BF16)
        qte = sb("qte", [128, 4, CH], BF16)
        ksum = sb("ksum", [128, 4], F32)
        kmT = sb("kmT", [128, 4, 32], BF16)
        gsb = sb("gsb", [128, 4, 32], F32)
        top8 = sb("top8", [128, 4, 8], F32)
        bqp = sb("bqp", [128, 4, 96], BF16)
        PT = [sb("PT%d" % i, [128, 512], BF16) for i in range(2)]
        rden = sb("rden", [128, 4], F32)
        ym = sb("ym", [128, 256], BF16)
        tp = pp("tp", [128, 1024], BF16)
        pz = [pp("pz%d" % i, [128, 512]) for i in range(3)]
        ps = [pp("ps%d" % i, [128, 512]) for i in range(2)]
        po = [pp("po%d" % i, [128, 512]) for i in range(2)]
        tpb = ps[1][:].bitcast(BF16)
        pzc = [0]

        pz_fixed = [False]

        def nz():
            if pz_fixed[0]:
                return 2
            b = pzc[0] % 3
            pzc[0] += 1
            return b

        for k in range(KD):
            sch.dma("pool", lambda e, k=k: e.dma_start(out=win[:, k, :], in_=w_in_ap[k * 128:(k + 1) * 128, :]),
                    writes=[("win", k)])
        for k in range(KD):
            sch.dma("pool", lambda e, k=k: e.dma_start(out=wout[:, k, :], in_=w_out_ap[k * 128:(k + 1) * 128, :]),
                    writes=[("wout", k)])
        WIN = [("win", k) for k in range(KD)]
        WOUT = [("wout", k) for k in range(KD)]
        if "X1" not in DBG_SKIP:
            sch.op("pool", lambda e: e.memset(pwbd[:], 0.0), writes=["pwbd"])
            for pr in range(2):
                for a in range(2):
                    sch.dma("pool", lambda e, pr=pr, a=a: e.dma_start(
                        out=pwbd[a * 64:(a + 1) * 64, pr, a * 64:(a + 1) * 64], in_=pool_w_ap[2 * pr + a]),
                        reads=["pwbd"], writes=["pwbd"])
            sch.dma("sp", lambda e: e.dma_start(out=pscale[:], in_=pool_scale_ap.rearrange("(r p) -> p r", p=128),
                                                allow_slow_non_contiguous=True),
                    writes=["pscale"])
        sch.dma("sp", lambda e: e.dma_start(out=gbc[:], in_=g_ap.partition_broadcast(128)), writes=["gbc"])
        sch.dma("sp", lambda e: e.dma_start(out=ident[:], in_=cst["c_ident"]), writes=["ident"])
        if "X1" not in DBG_SKIP:
            sch.dma("sp", lambda e: e.dma_start(out=tri[:], in_=cst["c_tri"]), writes=["tri"])
            sch.dma("sp", lambda e: e.dma_start(out=band[:], in_=cst["c_band"]), writes=["band"])
            sch.dma("sp", lambda e: e.dma_start(out=dm[:], in_=cst["c_dm"]), writes=["dm"])
            sch.dma("sp", lambda e: e.dma_start(out=xz[:], in_=cst["c_xz"]), writes=["xz"])
            for h in range(4):
                sch.dma("sp", lambda e, h=h: e.dma_start(out=kte[64:96, h, :], in_=cst["c_onehot"][:, 0:nchunks * CH]),
                        writes=[("kte_oh", h)])
        sch.op("dve", lambda e: e.memset(epsb[:], EPS), writes=["epsb"])
        if "X1" not in DBG_SKIP:
            sch.op("pool", lambda e: e.memset(vc[:], 1.0), writes=["vc_init"])
            sch.op("pool", lambda e: e.memset(qte[:], 0.0), writes=["qte_init"])
            sch.op("pool", lambda e: e.memset(bqp[:], 0.0), writes=["bqp_init"])
            sch.op("pool", lambda e: e.memset(gsb[:], -1e30), writes=["gsb_init"])
            sch.op("pool", lambda e: e.memset(kmT[:], 0.0), writes=["kmT_init"])
            sch.op("pool", lambda e: e.memset(Rst[:], 0.0), writes=["Rst"])
            sch.op("pool", lambda e: e.memset(Rb[:], 0.0), writes=["Rb"])
            sch.op("pool", lambda e: e.memset(u[2][:], 0.0), writes=[("u", 2)])
        if DBG_SKIP:
            sch.op("pool", lambda e: e.memset(yT[:], 0.0), writes=["yT_init"])

        inplace = (x_in.tensor.name == x_out.tensor.name)
        if not inplace:
            sch.dma("sp", lambda e: e.dma_start(out=x_out[0:nchunks * CH, :], in_=x_in[0:nchunks * CH, :]),
                    writes=["xcopy"])

        def load_x(c):
            src = x_out[c * CH:(c + 1) * CH, :].rearrange("(s p) d -> p s d", p=128)
            sch.dma("sp", lambda e: e.dma_start(out=xt[:], in_=src), reads=["xcopy"], writes=["xt"])

        def norm_T(c):
            for s in range(2):
                sch.op("act", lambda e, s=s: e.activation(out=hb0[:], in_=xt[:, s, :], func=AF.Square,
                                                         accum_out=ss[:, s:s + 1]),
                       reads=["xt"], writes=[("hb", 0), ("ss", s)])
            sch.op("act", lambda e: e.activation(out=rstd[:], in_=ss[:], func=AF.Sqrt, bias=epsb[:],
                                                 scale=1.0 / D),
                   reads=[("ss", 0), ("ss", 1), "epsb"], writes=["rstd0"])
            sch.op("dve", lambda e: e.reciprocal(out=rstd[:], in_=rstd[:]), reads=["rstd0"], writes=["rstd"])
            for s in range(2):
                sch.op("dve", lambda e, s=s: e.scalar_tensor_tensor(
                    out=hb[s][:], in0=xt[:, s, :], scalar=rstd[:, s:s + 1], in1=gbc[:],
                    op0=ALU.mult, op1=ALU.mult),
                    reads=["xt", "rstd", "gbc"], writes=[("hb", 0)])

                def tr(e, s=s):
                    inst = None
                    for k in range(KD):
                        inst = e.transpose(out=tp[:, k * 128:(k + 1) * 128],
                                           in_=hb[s][:, k * 128:(k + 1) * 128], identity=ident[:])
                    return inst
                sch.op("pe", tr, reads=[("hb", 0), "ident"], writes=["tp"])
                sch.op("act", lambda e, s=s: e.copy(
                    out=hT[:, :, s * 128:(s + 1) * 128], in_=tp[:].rearrange("p (k t) -> p k t", k=KD)),
                    reads=["tp"], writes=[("hT", s)])

        def chunk(c):
            nonlocal sch
            tok0 = c * CH
            csrc = cst["c_cs"][tok0:tok0 + CH, :].rearrange("(s p) d -> p s d", p=128)
            sch.dma("sp", lambda e, csrc=csrc: e.dma_start(out=cs[:], in_=csrc), writes=["cs"])
            if c + 1 < nchunks:
                load_x(c + 1)

            def proj_tok(e, pb, s, col0, ncol, dst0=0):
                inst = None
                for k in range(KD):
                    inst = e.matmul(pz[pb][:, dst0:dst0 + ncol], lhsT=hT[:, k, s * 128:(s + 1) * 128],
                                    rhs=win[:, k, col0:col0 + ncol], start=(k == 0), stop=(k == KD - 1))
                return inst

            ucur = [(2 * c) % 3, (2 * c + 1) % 3]
            uprev = [(2 * c + 2) % 3, (2 * c) % 3]
            def p_v(s):
                b = nz()
                sch.op("pe", lambda e: proj_tok(e, b, s, O_RV, 512), reads=[("hT", s)] + WIN, writes=[("pz", b)])
                sch.op("act", lambda e: e.copy(out=vret[s][:], in_=pz[b][:]), reads=[("pz", b)],
                       writes=[("vret", s)])

            def p_g(s):
                b = nz()
                sch.op("pe", lambda e: proj_tok(e, b, s, O_RG, 512), reads=[("hT", s)] + WIN, writes=[("pz", b)])
                sch.op("act", lambda e: e.activation(out=sg[s][:], in_=pz[b][:], func=AF.Silu),
                       reads=[("pz", b)], writes=[("sg", s)])

            def p_uv(s):
                b = nz()

                def puv(e):
                    proj_tok(e, b, s, O_PU, 256, 0)
                    return proj_tok(e, b, s, O_MV, 256, 256)
                sch.op("pe", puv, reads=[("hT", s)] + WIN, writes=[("pz", b)])
                sch.op("dve", lambda e: e.tensor_copy(out=u[ucur[s]][:], in_=pz[b][:, 0:256]),
                       reads=[("pz", b)], writes=[("u", ucur[s])])
                sch.op("act", lambda e: e.copy(
                    out=vc[:, 2 * c + s, :, 0:64], in_=pz[b][:, 256:512].rearrange("p (h d) -> p h d", h=4)),
                    reads=[("pz", b), "vc_init"], writes=[("vc", 2 * c + s)])

            def p_qk(s):
                b = nz()
                sch.op("pe", lambda e: proj_tok(e, b, s, O_RQ, 512), reads=[("hT", s)] + WIN, writes=[("pz", b)])
                return b

            def rot(s, b):
                z4 = pz[b][:].rearrange("p (a h d) -> p a h d", a=2, h=4)
                cosb = cs[:, s, 0:128].rearrange("p (a o d) -> p a o d", a=2, o=1).broadcast_to([128, 2, 4, 64])
                z5 = pz[b][:].rearrange("p (a h d two) -> p a h d two", a=2, h=4, two=2)
                t25 = t2[:].rearrange("p (a h d two) -> p a h d two", a=2, h=4, two=2)
                sn5 = cs[:, s, 128:256].rearrange("p (a o d two) -> p a o d two", a=2, o=1, two=2)
                for par in range(2):
                    sch.op("dve", lambda e, par=par: e.tensor_tensor(
                        out=t25[:, :, :, :, par], in0=z5[:, :, :, :, 1 - par],
                        in1=sn5[:, :, :, :, par].broadcast_to([128, 2, 4, 32]), op=ALU.mult),
                        reads=[("pz", b), "cs"], writes=[("t2", par)])
                sch.op("dve", lambda e: e.tensor_tensor(out=z4, in0=z4, in1=cosb, op=ALU.mult),
                       reads=[("pz", b), "cs"], writes=[("pz", b)])
                sch.op("dve", lambda e: e.tensor_tensor(out=qkr[:], in0=pz[b][:], in1=t2[:], op=ALU.add),
                       reads=[("pz", b), ("t2", 0), ("t2", 1)], writes=["qkr"])
                xzb = xz[:, s, :].rearrange("p (a h o) -> p a h o", a=2, o=1).broadcast_to([128, 2, 4, 64])
                sch.op("dve", lambda e: e.tensor_tensor(
                    out=qxz[s][:].rearrange("p (a h d) -> p a h d", a=2, h=4),
                    in0=qkr[:].rearrange("p (a h d) -> p a h d", a=2, h=4), in1=xzb, op=ALU.mult),
                    reads=["qkr", "xz"], writes=[("qxz", s)])

            def tr_(s):
                def trr(e):
                    inst = None
                    srcs = [qkr[:, 0:128], qkr[:, 128:256], qxz[s][:, 0:128], qxz[s][:, 128:256],
                            qkr[:, 256:384], qkr[:, 384:512]]
                    for j, sr in enumerate(srcs):
                        inst = e.transpose(out=tp[:, j * 128:(j + 1) * 128], in_=sr, identity=ident[:])
                    return inst
                sch.op("pe", trr, reads=["qkr", ("qxz", s), "ident"], writes=["tp"])
                sch.op("act", lambda e: e.copy(
                    out=rT[:, :, s * 128:(s + 1) * 128],
                    in_=tp[:, 0:768].rearrange("p (j t) -> p j t", j=6)),
                    reads=["tp"], writes=[("rT", s)])

            bq0 = p_qk(0)
            bq1 = p_qk(1)
            rot(0, bq0)
            p_v(0)
            tr_(0)
            rot(1, bq1)
            p_g(0)
            p_uv(0)
            p_v(1)
            tr_(1)
            p_g(1)
            p_uv(1)

            if "X4" not in DBG_SKIP:
                for h in range(4):
                    b = nz()

                    def pqk(e, b=b, h=h):
                        inst = None
                        for k in range(KD):
                            e.matmul(pz[b][0:64, 0:256], lhsT=win[:, k, O_MQ + h * 64:O_MQ + (h + 1) * 64],
                                     rhs=hT[:, k, :], start=(k == 0), stop=(k == KD - 1))
                        for k in range(KD):
                            inst = e.matmul(pz[b][0:64, 256:512], lhsT=win[:, k, O_MK + h * 64:O_MK + (h + 1) * 64],
                                            rhs=hT[:, k, :], start=(k == 0), stop=(k == KD - 1))
                        return inst
                    sch.op("pe", pqk, reads=[("hT", 0), ("hT", 1)] + WIN, writes=[("pz", b)])
                    sch.op("dve", lambda e, b=b, h=h: e.tensor_scalar(
                        out=qte[0:64, h, :], in0=pz[b][0:64, 0:256], scalar1=0.125, scalar2=None, op0=ALU.mult),
                        reads=[("pz", b), "qte_init"], writes=[("qte_q", h)])
                    sch.op("act", lambda e, b=b, h=h: e.activation(
                        out=kte[0:64, h, tok0:tok0 + CH], in_=pz[b][0:64, 256:512], func=AF.Identity,
                        accum_out=ksum[0:64, h:h + 1]),
                        reads=[("pz", b)], writes=[("kte", h), ("ksum", h)])
                    sch.op("dve", lambda e, h=h: e.tensor_scalar(
                        out=kmT[0:64, h, c:c + 1], in0=ksum[0:64, h:h + 1], scalar1=1.0 / CH, scalar2=None,
                        op0=ALU.mult),
                        reads=[("ksum", h), "kmT_init"], writes=[("kmT", h)])

            real = sch
            recA = _Rec()
            sch = recA
            pz_fixed[0] = True
            if c + 1 < nchunks:
                norm_T(c + 1)

            if "ret" not in DBG_SKIP:
                for h in range(4):
                    pr, base = h // 2, 64 * (h % 2)
                    ib = h % 2

                    def inner(e, ib=ib, pr=pr, base=base):
                        e.matmul(pz[2][:, 0:256], lhsT=rT[base:base + 64, 4 + pr, 0:128],
                                 rhs=rT[base:base + 64, pr, 0:256], start=True, stop=True)
                        return e.matmul(pz[2][:, 256:384], lhsT=rT[base:base + 64, 4 + pr, 128:256],
                                        rhs=rT[base:base + 64, pr, 128:256], start=True, stop=True)
                    sch.op("pe", inner, reads=[("rT", 0), ("rT", 1)], writes=[("pz", 2)])
                    sch.op("dve", lambda e, h=h, ib=ib: e.tensor_tensor(
                        out=inT[ib][:, 0:256], in0=pz[2][:, 0:256], in1=dm[:, h, :], op=ALU.mult),
                        reads=[("pz", 2), "dm"], writes=[("inT", ib, 0)])
                    sch.op("dve", lambda e, h=h, ib=ib: e.tensor_tensor(
                        out=inT[ib][:, 256:384], in0=pz[2][:, 256:384], in1=dm[:, h, 0:128], op=ALU.mult),
                        reads=[("pz", 2), "dm"], writes=[("inT", ib, 1)])

                    def outm(e, h=h, pr=pr, base=base, ib=ib):
                        hc = slice(h * 128, (h + 1) * 128)
                        e.matmul(pz[0][:, hc], lhsT=inT[ib][:, 0:128], rhs=vret[0][:, hc], start=True,
                                 stop=False)
                        e.matmul(pz[0][:, hc], lhsT=rT[base:base + 64, 2 + pr, 0:128],
                                 rhs=Rb[base:base + 64, pr, :], start=False, stop=True)
                        e.matmul(pz[1][:, hc], lhsT=inT[ib][:, 128:256], rhs=vret[0][:, hc], start=True,
                                 stop=False)
                        e.matmul(pz[1][:, hc], lhsT=inT[ib][:, 256:384], rhs=vret[1][:, hc], start=False,
                                 stop=False)
                        return e.matmul(pz[1][:, hc], lhsT=rT[base:base + 64, 2 + pr, 128:256],
                                        rhs=Rb[base:base + 64, pr, :], start=False, stop=True)
                    sch.op("pe", outm, reads=[("inT", ib, 0), ("inT", ib, 1), ("vret", 0), ("vret", 1), ("rT", 0),
                                              ("rT", 1), "Rb"],
                           writes=[("pz", 0), ("pz", 1)])
                for s in range(2):
                    for h in range(4):
                        j = s * 4 + h
                        sch.op("dve", lambda e, s=s, h=h, j=j: e.bn_stats(
                            out=st6[:, j, :], in_=pz[s][:, h * 128:(h + 1) * 128]),
                            reads=[("pz", s)], writes=[("st6", j)])
                        sch.op("dve", lambda e, j=j: e.bn_aggr(out=mv[:, j, :], in_=st6[:, j, :]),
                               reads=[("st6", j)], writes=[("mv", j)])
                MV = [("mv", j) for j in range(8)]
                sch.op("act", lambda e: e.activation(out=grs[:], in_=mv[:, :, 1], func=AF.Sqrt, bias=epsb[:], scale=1.0),
                       reads=MV + ["epsb"], writes=["grs0"])
                sch.op("dve", lambda e: e.reciprocal(out=grs[:], in_=grs[:]), reads=["grs0"], writes=["grs"])
                sch.op("dve", lambda e: e.scalar_tensor_tensor(out=gnb[:], in0=mv[:, :, 0], scalar=-1.0, in1=grs[:],
                                                                op0=ALU.mult, op1=ALU.mult),
                       reads=MV + ["grs"], writes=["gnb"])
                for s in range(2):
                    for h in range(4):
                        j = s * 4 + h
                        sch.op("act", lambda e, s=s, h=h, j=j: e.activation(
                            out=on[:, h * 128:(h + 1) * 128], in_=pz[s][:, h * 128:(h + 1) * 128], func=AF.Identity,
                            bias=gnb[:, j:j + 1], scale=grs[:, j:j + 1]),
                            reads=[("pz", s), "grs", "gnb"], writes=[("on", h), ("t2", 0), ("t2", 1)])
                    sch.op("dve", lambda e, s=s: e.tensor_tensor(out=yret[:], in0=on[:], in1=sg[s][:], op=ALU.mult),
                           reads=[("on", h) for h in range(4)] + [("sg", s), ("t2", 0), ("t2", 1)], writes=["qkr"])

                    def try_(e):
                        inst = None
                        for k in range(4):
                            inst = e.transpose(out=tp[:, k * 128:(k + 1) * 128], in_=yret[:, k * 128:(k + 1) * 128],
                                               identity=ident[:])
                        return inst
                    sch.op("pe", try_, reads=["qkr", "ident"], writes=["tp"])
                    sch.op("act", lambda e, s=s: e.copy(
                        out=yT[:, 0:4, s * 128:(s + 1) * 128], in_=tp[:, 0:512].rearrange("p (k t) -> p k t", k=4)),
                        reads=["tp"], writes=[("yT", "ret", s)])
                for pr in range(2):
                    b = nz()

                    def stm(e, b=b, pr=pr):
                        inst = None
                        for a in range(2):
                            h = 2 * pr + a
                            for s in range(2):
                                inst = e.matmul(pz[b][:, a * 128:(a + 1) * 128],
                                                lhsT=qxz[s][:, 256 + pr * 128:256 + (pr + 1) * 128],
                                                rhs=vret[s][:, h * 128:(h + 1) * 128], start=(s == 0), stop=(s == 1))
                        return inst
                    sch.op("pe", stm, reads=[("qxz", 0), ("qxz", 1), ("vret", 0), ("vret", 1)], writes=[("pz", b)])
                    for a in range(2):
                        h = 2 * pr + a
                        gC = float(np.float32(GAMMAS[h]) ** np.float32(CH))
                        sch.op("dve", lambda e, b=b, pr=pr, a=a, gC=gC: e.scalar_tensor_tensor(
                            out=Rst[a * 64:(a + 1) * 64, pr, :], in0=Rst[a * 64:(a + 1) * 64, pr, :], scalar=gC,
                            in1=pz[b][a * 64:(a + 1) * 64, a * 128:(a + 1) * 128], op0=ALU.mult, op1=ALU.add),
                            reads=[("pz", b), "Rst"], writes=["Rst"])
                sch.op("act", lambda e: e.copy(out=Rb[:], in_=Rst[:]), reads=["Rst"], writes=["Rb"])

            if "pool" not in DBG_SKIP:
                for s in range(2):
                    for pr in range(2):
                        b = nz()

                        def poolm(e, b=b, s=s, pr=pr):
                            inst = None
                            first = (c == 0 and s == 0)
                            for a in range(2):
                                gi = 2 * pr + a
                                inst = e.matmul(pz[b][:, a * 128:(a + 1) * 128],
                                                lhsT=u[ucur[s]][:, pr * 128:(pr + 1) * 128],
                                                rhs=band[:, (8 + gi) if first else gi, :], start=True, stop=first)
                                if not first:
                                    inst = e.matmul(pz[b][:, a * 128:(a + 1) * 128],
                                                    lhsT=u[uprev[s]][:, pr * 128:(pr + 1) * 128],
                                                    rhs=band[:, 4 + gi, :], start=False, stop=True)
                            return inst
                        sch.op("pe", poolm, reads=[("u", ucur[s]), ("u", uprev[s]), "band"], writes=[("pz", b)])
                        sch.op("act", lambda e, b=b, s=s, pr=pr: e.copy(
                            out=pooledT[0:64, pr, s * 128:(s + 1) * 128], in_=pz[b][0:64, 0:128]),
                            reads=[("pz", b)], writes=[("pooledT", pr, s, 0)])
                        sch.op("dve", lambda e, b=b, s=s, pr=pr: e.tensor_copy(
                            out=pooledT[64:128, pr, s * 128:(s + 1) * 128], in_=pz[b][64:128, 128:256]),
                            reads=[("pz", b)], writes=[("pooledT", pr, s, 1)])
                for pr in range(2):
                    b = nz()
                    sch.op("pe", lambda e, b=b, pr=pr: e.matmul(pz[b][:, 0:256], lhsT=pwbd[:, pr, :],
                                                                rhs=pooledT[:, pr, :], start=True, stop=True),
                           reads=[("pooledT", pr, s, a) for s in range(2) for a in range(2)] + ["pwbd"],
                           writes=[("pz", b)])
                    sch.op("act", lambda e, b=b, pr=pr: e.activation(
                        out=yT[:, 4 + pr, :], in_=pz[b][:, 0:256], func=AF.Identity, scale=pscale[:, pr:pr + 1]),
                        reads=[("pz", b), "pscale"], writes=[("yT", "pool", pr)])

            recB = _Rec()
            sch = recB
            if "moba" not in DBG_SKIP:
                if c > 0:
                    for s in range(2):
                        def gatem(e, s=s):
                            inst = None
                            for h in range(4):
                                inst = e.matmul(ps[0][:, h * 32:(h + 1) * 32], lhsT=qte[0:64, h, s * 128:(s + 1) * 128],
                                                rhs=kmT[0:64, h, :], start=True, stop=True)
                            return inst
                        sch.op("pe", gatem, reads=[("qte_q", h) for h in range(4)] + [("kmT", h) for h in range(4)],
                               writes=[("ps", 0)])
                        sch.op("dve", lambda e: e.tensor_copy(
                            out=gsb[:, :, 0:c], in_=ps[0][:, 0:128].rearrange("p (h j) -> p h j", h=4)[:, :, 0:c]),
                            reads=[("ps", 0), "gsb_init"], writes=["gsb"])
                        for h in range(4):
                            sch.op("dve", lambda e, h=h: e.max(out=top8[:, h, :], in_=gsb[:, h, :]),
                                   reads=["gsb"], writes=[("top8", h)])
                            sch.op("dve", lambda e, h=h: e.tensor_scalar(
                                out=bqp[:, h, 64:64 + c], in0=gsb[:, h, 0:c], scalar1=top8[:, h, 2:3], scalar2=NEG,
                                op0=ALU.is_lt, op1=ALU.mult),
                                reads=["gsb", ("top8", h), "bqp_init"], writes=[("bqp", h)])

                        def trb(e):
                            inst = None
                            for h in range(4):
                                inst = e.transpose(out=tpb[0:96, h * 128:(h + 1) * 128], in_=bqp[:, h, :],
                                                   identity=ident[:])
                            return inst
                        sch.op("pe", trb, reads=[("bqp", h) for h in range(4)] + ["ident"], writes=[("ps", 1)])
                        sch.op("act", lambda e, s=s: e.copy(
                            out=qte[64:96, :, s * 128:(s + 1) * 128],
                            in_=tpb[64:96, 0:512].rearrange("p (h t) -> p h t", h=4)),
                            reads=[("ps", 1), "qte_init"], writes=[("qte_b", s)])
                QTE = [("qte_b", 0), ("qte_b", 1)]
                steps = [(h, j) for h in range(4) for j in range(c + 1)]

                def rec_qk(i):
                    h, j = steps[i]
                    own = (j == c)
                    sbk = i % 2

                    def qk(e):
                        k0 = j * CH
                        if not own:
                            e.matmul(ps[sbk][:, 0:256], lhsT=kte[0:96, h, k0:k0 + 128], rhs=qte[0:96, h, :],
                                     start=True, stop=True)
                            return e.matmul(ps[sbk][:, 256:512], lhsT=kte[0:96, h, k0 + 128:k0 + 256],
                                            rhs=qte[0:96, h, :], start=True, stop=True)
                        e.matmul(ps[sbk][:, 0:256], lhsT=kte[0:96, h, k0:k0 + 128], rhs=qte[0:96, h, :],
                                 start=True, stop=False)
                        e.matmul(ps[sbk][:, 0:128], lhsT=ident[:], rhs=tri[:], start=False, stop=True)
                        e.matmul(ps[sbk][:, 256:384], lhsT=kte[0:96, h, k0 + 128:k0 + 256],
                                 rhs=qte[0:96, h, 128:256], start=True, stop=False)
                        return e.matmul(ps[sbk][:, 256:384], lhsT=ident[:], rhs=tri[:], start=False, stop=True)
                    sch.op("pe", qk, reads=[("kte", h), ("kte_oh", h), ("qte_q", h), "ident", "tri", "qte_init"] + QTE,
                           writes=[("ps", sbk)])

                def rec_exp(i):
                    h, j = steps[i]
                    sbk = pbuf = i % 2
                    ncol = 384 if j == c else 512
                    sch.op("act", lambda e: e.activation(out=PT[pbuf][:, 0:ncol], in_=ps[sbk][:, 0:ncol], func=AF.Exp),
                           reads=[("ps", sbk)], writes=[("PT", pbuf)])

                def rec_pv(i):
                    h, j = steps[i]
                    own = (j == c)
                    pbuf = i % 2
                    hs = slice(h * 65, (h + 1) * 65)

                    def pv(e):
                        inst = None
                        first = (j == 0)
                        if not own:
                            for kc in range(2):
                                for s in range(2):
                                    inst = e.matmul(po[s][:, hs],
                                                    lhsT=PT[pbuf][:, kc * 256 + s * 128:kc * 256 + (s + 1) * 128],
                                                    rhs=vc[:, 2 * j + kc, h, :],
                                                    start=(first and kc == 0), stop=False)
                            return inst
                        e.matmul(po[0][:, hs], lhsT=PT[pbuf][:, 0:128], rhs=vc[:, 2 * j, h, :],
                                 start=first, stop=True)
                        e.matmul(po[1][:, hs], lhsT=PT[pbuf][:, 128:256], rhs=vc[:, 2 * j, h, :],
                                 start=first, stop=False)
                        return e.matmul(po[1][:, hs], lhsT=PT[pbuf][:, 256:384], rhs=vc[:, 2 * j + 1, h, :],
                                        start=False, stop=True)
                    sch.op("pe", pv, reads=[("PT", pbuf), ("vc", 2 * j), ("vc", 2 * j + 1)],
                           writes=[("po", 0), ("po", 1)])

                rec_qk(0)
                for i in range(len(steps)):
                    rec_exp(i)
                    if i + 1 < len(steps):
                        rec_qk(i + 1)
                    rec_pv(i)
                sch = real
                pz_fixed[0] = False
                _merge_play(real, recB.items, recA.items)
                for s in range(2):
                    po3 = po[s][:, 0:260].rearrange("p (h d) -> p h d", h=4)
                    sch.op("dve", lambda e, po3=po3: e.reciprocal(out=rden[:], in_=po3[:, :, 64]),
                           reads=[("po", s)], writes=["rden"])
                    sch.op("dve", lambda e, po3=po3: e.tensor_tensor(
                        out=ym[:].rearrange("p (h d) -> p h d", h=4), in0=po3[:, :, 0:64],
                        in1=rden[:].rearrange("p (h o) -> p h o", o=1).broadcast_to([128, 4, 64]), op=ALU.mult),
                        reads=[("po", s), "rden"], writes=["ym"])

                    def trm(e):
                        e.transpose(out=tp[:, 0:128], in_=ym[:, 0:128], identity=ident[:])
                        return e.transpose(out=tp[:, 128:256], in_=ym[:, 128:256], identity=ident[:])
                    sch.op("pe", trm, reads=["ym", "ident"], writes=["tp"])
                    sch.op("act", lambda e, s=s: e.copy(
                        out=yT[:, 6:8, s * 128:(s + 1) * 128], in_=tp[:, 0:256].rearrange("p (k t) -> p k t", k=2)),
                        reads=["tp"], writes=[("yT", "moba", s)])

            YT = [("yT", "ret", 0), ("yT", "ret", 1), ("yT", "pool", 0), ("yT", "pool", 1),
                  ("yT", "moba", 0), ("yT", "moba", 1)]
            for s in range(2):
                for hlf in range(2):
                    b = nz()

                    def wo(e, b=b, s=s, hlf=hlf):
                        inst = None
                        for k in range(KD):
                            inst = e.matmul(pz[b][:], lhsT=yT[:, k, s * 128:(s + 1) * 128],
                                            rhs=wout[:, k, hlf * 512:(hlf + 1) * 512], start=(k == 0),
                                            stop=(k == KD - 1))
                        return inst
                    sch.op("pe", wo, reads=YT + WOUT + ["yT_init"], writes=[("pz", b)])
                    sch.op("dve", lambda e, b=b, s=s, hlf=hlf: e.tensor_copy(
                        out=xt[:, s, hlf * 512:(hlf + 1) * 512], in_=pz[b][:]),
                        reads=[("pz", b), "xt"], writes=["xt"])
            dst = x_out[tok0:tok0 + CH, :].rearrange("(s p) d -> p s d", p=128)
            sch.dma("pool", lambda e, dst=dst: e.dma_start(out=dst, in_=xt[:], accum_op=ALU.add),
                    reads=["xt", "xcopy"], writes=[("xout", c)])

        load_x(0)
        norm_T(0)
        for c in range(nchunks):
            chunk(c)

    ctx.phase(body)


def ple_phase(ctx, x_in, x_out, p_ap, g_ap, wg_ap, wp_ap, ident_dram, gfin_ap, ntiles):
    final = gfin_ap is not None

    def body(sch, sb, pp):
        wg = sb("wg", [128, KD, D], BF16)
        wp = sb("wp", [128, 2, D], BF16)
        gbc = sb("gbc", [128, D], F32)
        gfb = sb("gfb", [128, D], F32)
        ident = sb("ident", [128, 128], BF16)
        epsb = sb("epsb", [128, 1], F32)
        xt = [sb("xt%d" % i, [128, NSUB, D], F32) for i in range(2)]
        pt = [sb("pt%d" % i, [128, NSUB, PLE], F32) for i in range(2)]
        sq = sb("sq", [128, D], BF16)
        ss = sb("ss", [128, NSUB], F32)
        rstd = sb("rstd", [128, NSUB], F32)
        ss2 = sb("ss2", [128, NSUB], F32)
        rstd2 = sb("rstd2", [128, NSUB], F32)
        hb = [sb("hb%d" % i, [128, D], BF16) for i in range(2)]
        pb16 = [sb("pb16_%d" % i, [128, PLE], BF16) for i in range(2)]
        hT = sb("hT", [128, KD, TT], BF16)
        pT = sb("pT", [128, 2, TT], BF16)
        sig = [sb("sig%d" % i, [128, 512], F32) for i in range(2)]
        tmp = [sb("tmp%d" % i, [128, 512], F32) for i in range(2)]
        ot = [sb("ot%d" % i, [128, NSUB, D], F32) for i in range(2)] if final else None
        tp = [pp("tp%d" % i, [128, D], BF16) for i in range(2)]
        tpp = pp("tpp", [128, 512], BF16)
        pg = [pp("pg%d" % i, [128, 512]) for i in range(2)]
        pj = [pp("pj%d" % i, [128, 512]) for i in range(2)]

        for k in range(KD):
            sch.dma("pool", lambda e, k=k: e.dma_start(out=wg[:, k, :], in_=wg_ap[k * 128:(k + 1) * 128, :]),
                    writes=[("wg", k)])
        for k in range(2):
            sch.dma("pool", lambda e, k=k: e.dma_start(out=wp[:, k, :], in_=wp_ap[k * 128:(k + 1) * 128, :]),
                    writes=[("wp", k)])
        sch.dma("sp", lambda e: e.dma_start(out=gbc[:], in_=g_ap.partition_broadcast(128)), writes=["gbc"])
        if final:
            sch.dma("sp", lambda e: e.dma_start(out=gfb[:], in_=gfin_ap.partition_broadcast(128)), writes=["gfb"])
        sch.dma("sp", lambda e: e.dma_start(out=ident[:], in_=ident_dram), writes=["ident"])
        sch.op("dve", lambda e: e.memset(epsb[:], EPS), writes=["epsb"])
        WG = [("wg", k) for k in range(KD)]
        WP = [("wp", k) for k in range(2)]

        def load(t):
            b = t % 2
            src = x_in[t * TT:(t + 1) * TT, :].rearrange("(s p) d -> p s d", p=128)
            sch.dma("sp", lambda e: e.dma_start(out=xt[b][:], in_=src), writes=[("xt", b)])
            psrc = p_ap[t * TT:(t + 1) * TT, :].rearrange("(s p) d -> p s d", p=128)
            sch.dma("sp", lambda e: e.dma_start(out=pt[b][:], in_=psrc), writes=[("pt", b)])

        def norm_T(t):
            b = t % 2
            for s in range(NSUB):
                sch.op("act", lambda e, s=s: e.activation(out=sq[:], in_=xt[b][:, s, :], func=AF.Square,
                                                         accum_out=ss[:, s:s + 1]),
                       reads=[("xt", b)], writes=["sq", ("ss", s)])
            sch.op("act", lambda e: e.activation(out=rstd[:], in_=ss[:], func=AF.Sqrt, bias=epsb[:],
                                                 scale=1.0 / D),
                   reads=[("ss", s) for s in range(NSUB)] + ["epsb"], writes=["rstd0"])
            sch.op("dve", lambda e: e.reciprocal(out=rstd[:], in_=rstd[:]), reads=["rstd0"], writes=["rstd"])
            for s in range(NSUB):
                hbuf = hb[s % 2]
                tpb = tp[s % 2]
                pbuf = pb16[s % 2]
                sch.op("dve", lambda e, s=s, hbuf=hbuf: e.scalar_tensor_tensor(
                    out=hbuf[:], in0=xt[b][:, s, :], scalar=rstd[:, s:s + 1], in1=gbc[:],
                    op0=ALU.mult, op1=ALU.mult),
                    reads=[("xt", b), "rstd", "gbc"], writes=[("hb", s % 2)])
                sch.op("pool", lambda e, s=s, pbuf=pbuf: e.tensor_copy(out=pbuf[:], in_=pt[b][:, s, :]),
                       reads=[("pt", b)], writes=[("pb16", s % 2)])

                def tr(e, hbuf=hbuf, tpb=tpb):
                    inst = None
                    for k in range(KD):
                        inst = e.transpose(out=tpb[:, k * 128:(k + 1) * 128],
                                           in_=hbuf[:, k * 128:(k + 1) * 128], identity=ident[:])
                    return inst
                sch.op("pe", tr, reads=[("hb", s % 2), "ident"], writes=[("tp", s % 2)])
                sch.op("act", lambda e, s=s, tpb=tpb: e.copy(
                    out=hT[:, :, s * 128:(s + 1) * 128], in_=tpb[:].rearrange("p (k t) -> p k t", k=KD)),
                    reads=[("tp", s % 2)], writes=[("hT", s)])

                def trp(e, pbuf=pbuf):
                    e.transpose(out=tpp[:, 0:128], in_=pbuf[:, 0:128], identity=ident[:])
                    return e.transpose(out=tpp[:, 128:256], in_=pbuf[:, 128:256], identity=ident[:])
                sch.op("pe", trp, reads=[("pb16", s % 2), "ident"], writes=["tpp"])
                sch.op("act", lambda e, s=s: e.copy(
                    out=pT[:, :, s * 128:(s + 1) * 128], in_=tpp[:, 0:256].rearrange("p (k t) -> p k t", k=2)),
                    reads=["tpp"], writes=[("pT", s)])

        def compute(t):
            xb = t % 2
            for s in range(NSUB):
                for hlf in range(2):
                    b = (s * 2 + hlf) % 2
                    cs_ = slice(hlf * 512, (hlf + 1) * 512)

                    def mm(e, s=s, b=b, cs_=cs_):
                        inst = None
                        for k in range(KD):
                            e.matmul(pg[b][:], lhsT=hT[:, k, s * 128:(s + 1) * 128], rhs=wg[:, k, cs_],
                                     start=(k == 0), stop=(k == KD - 1))
                        for k in range(2):
                            inst = e.matmul(pj[b][:], lhsT=pT[:, k, s * 128:(s + 1) * 128], rhs=wp[:, k, cs_],
                                            start=(k == 0), stop=(k == 1))
                        return inst
                    sch.op("pe", mm, reads=[("hT", s), ("pT", s)] + WG + WP, writes=[("pg", b), ("pj", b)])
                    sch.op("act", lambda e, b=b: e.activation(out=sig[b][:], in_=pg[b][:], func=AF.Sigmoid),
                           reads=[("pg", b)], writes=[("sig", b)])
                    sch.op("dve", lambda e, b=b: e.tensor_tensor(out=tmp[b][:], in0=pj[b][:], in1=sig[b][:],
                                                                  op=ALU.mult),
                           reads=[("pj", b), ("sig", b)], writes=[("tmp", b)])
                    sch.op("pool", lambda e, b=b, s=s, cs_=cs_: e.tensor_tensor(
                        out=xt[xb][:, s, cs_], in0=xt[xb][:, s, cs_], in1=tmp[b][:], op=ALU.add),
                        reads=[("tmp", b), ("xt", xb)], writes=[("xt", xb)])
            if final:
                for s in range(NSUB):
                    sch.op("act", lambda e, s=s: e.activation(out=sq[:], in_=xt[xb][:, s, :], func=AF.Square,
                                                             accum_out=ss2[:, s:s + 1]),
                           reads=[("xt", xb)], writes=["sq", ("ss2", s)])
                sch.op("act", lambda e: e.activation(out=rstd2[:], in_=ss2[:], func=AF.Sqrt, bias=epsb[:],
                                                     scale=1.0 / D),
                       reads=[("ss2", s) for s in range(NSUB)] + ["epsb"], writes=["rstd20"])
                sch.op("dve", lambda e: e.reciprocal(out=rstd2[:], in_=rstd2[:]), reads=["rstd20"],
                       writes=["rstd2"])
                for s in range(NSUB):
                    sch.op("dve", lambda e, s=s: e.scalar_tensor_tensor(
                        out=ot[xb][:, s, :], in0=xt[xb][:, s, :], scalar=rstd2[:, s:s + 1], in1=gfb[:],
                        op0=ALU.mult, op1=ALU.mult),
                        reads=[("xt", xb), "rstd2", "gfb"], writes=[("ot", xb)])

        def store(t):
            b = t % 2
            dst = x_out[t * TT:(t + 1) * TT, :].rearrange("(s p) d -> p s d", p=128)
            if final:
                sch.dma("sp", lambda e: e.dma_start(out=dst, in_=ot[b][:]), reads=[("ot", b)],
                        writes=[("xout", t)])
            else:
                sch.dma("sp", lambda e: e.dma_start(out=dst, in_=xt[b][:]), reads=[("xt", b)],
                        writes=[("xout", t)])

        load(0)
        norm_T(0)
        for t in range(ntiles):
            if t + 1 < ntiles:
                load(t + 1)
            compute(t)
            store(t)
            if t + 1 < ntiles:
                norm_T(t + 1)

    ctx.phase(body)


WEIGHT_SPECS = {
    "norm_ffn1": [DEPTH, D], "ffn1_w_gate": [DEPTH, D, DFF], "ffn1_w_up": [DEPTH, D, DFF],
    "ffn1_w_down": [DEPTH, DFF, D], "norm_mix": [DEPTH, D], "w_in": [DEPTH, D, 2560],
    "pool_w": [DEPTH, 4, 64, 64], "pool_scale": [DEPTH, 256], "w_out": [DEPTH, D, D],
    "norm_ffn2": [DEPTH, D], "ffn2_w_gate": [DEPTH, D, DFF], "ffn2_w_up": [DEPTH, D, DFF],
    "ffn2_w_down": [DEPTH, DFF, D], "norm_ple": [DEPTH, D], "ple_w_gate": [DEPTH, D, D],
    "ple_w_proj": [DEPTH, PLE, D], "norm_final": [D],
}


def build_nc(ntok=S, phases=None, depth=DEPTH):
    nc = bass.Bass("TRN2", target_bir_lowering=False)
    x = nc.dram_tensor("x", [ntok, D], F32, kind="ExternalInput").ap()
    p = nc.dram_tensor("p", [DEPTH, ntok, PLE], F32, kind="ExternalInput").ap()
    w = {k: nc.dram_tensor(k, shp, F32, kind="ExternalInput").ap() for k, shp in WEIGHT_SPECS.items()}
    cst = {}
    for k, (shp, dt) in CONST_SPECS.items():
        cst[k] = nc.dram_tensor(k, shp, dt, kind="ExternalInput").ap()
    out = nc.dram_tensor("out", [ntok, D], F32, kind="ExternalOutput").ap()
    xs = nc.dram_tensor("xs", [ntok, D], F32).ap()
    ident = cst["c_ident"]
    plan = []
    for i in range(depth):
        plan += [("ffn1", i), ("mix", i), ("ffn2", i), ("ple", i)]
    if phases is not None:
        plan = phases
    with contextlib.ExitStack() as stack:
        ctx = Ctx(nc, stack)
        cur = x
        for n, (ph, i) in enumerate(plan):
            last = (n == len(plan) - 1)
            dst = out if last else xs
            if ph == "ffn1":
                ffn_phase(ctx, "f1", cur, dst, w["norm_ffn1"][i], w["ffn1_w_gate"][i], w["ffn1_w_up"][i],
                          w["ffn1_w_down"][i], ident, ntok // TT)
            elif ph == "ffn2":
                ffn_phase(ctx, "f2", cur, dst, w["norm_ffn2"][i], w["ffn2_w_gate"][i], w["ffn2_w_up"][i],
                          w["ffn2_w_down"][i], ident, ntok // TT)
            elif ph == "mix":
                mixer_phase(ctx, cur, dst, w["norm_mix"][i], w["w_in"][i], w["pool_w"][i], w["pool_scale"][i],
                            w["w_out"][i], cst, ntok // CH)
            elif ph == "ple":
                fin = w["norm_final"] if (last and phases is None) else None
                ple_phase(ctx, cur, dst, p[i], w["norm_ple"][i], w["ple_w_gate"][i], w["ple_w_proj"][i], ident,
                          fin, ntok // TT)
            elif ph == "plefin":
                ple_phase(ctx, cur, dst, p[i], w["norm_ple"][i], w["ple_w_gate"][i], w["ple_w_proj"][i], ident,
                          w["norm_final"], ntok // TT)
            cur = dst
    return nc


_CONSTS = None


def kernel(**inputs):
    global _CONSTS
    if _CONSTS is None:
        _CONSTS = host_consts()
    x = np.ascontiguousarray(np.asarray(inputs["x"], dtype=np.float32))
    p = np.ascontiguousarray(np.asarray(inputs["p"], dtype=np.float32))
    shared = {k: np.ascontiguousarray(np.asarray(inputs[k], dtype=np.float32)) for k in WEIGHT_SPECS}
    shared.update(_CONSTS)
    nc = build_nc()
    in_maps = []
    for b in range(NCORES):
        m = dict(shared)
        m["x"] = x[b]
        m["p"] = np.ascontiguousarray(p[:, b])
        in_maps.append(m)
    res = run_bass_kernel_spmd(nc, in_maps, core_ids=list(range(NCORES)))
    return np.stack([np.asarray(r["out"], dtype=np.float32) for r in res.results], axis=0)
```
